# Optimizing a Trainium2 kernel written in Bass

```python
import math
import jax
import jax.numpy as jnp
from jax import lax
import numpy as np

D_MODEL = 1024
BATCH = 2
SEQ = 8192
DEPTH = 1
DEC_BATCH = 32
DEC_SEQ = 4
PAST_LEN = 8192
PAGE_SIZE = 128

HEAD_DIM = 64
H_NSA = 8
KV_NSA = 2
GRP = H_NSA // KV_NSA
H_FOX = 8
D_NSA = H_NSA * HEAD_DIM
D_FOX = H_FOX * HEAD_DIM
CMP_STRIDE = 16
CMP_LEN = 2 * CMP_STRIDE
CMP_HIDDEN = 128
SEL_BLOCK = 64
SUB_PER_SEL = SEL_BLOCK // CMP_STRIDE
TOP_N = 16
WINDOW = 512
Q_BLOCK = 128
T5_BUCKETS = 32
T5_EXACT = T5_BUCKETS // 2
T5_MAX_DIST = 1024
PLE_DIM = 256
FGATE_BIAS_INIT = 3.0
RMS_EPS = 1e-6
NEG_INF = -1e30
FORCED_SCORE = 1e9

COL_SIZES = (D_NSA, 2 * KV_NSA * HEAD_DIM, 2 * KV_NSA * HEAD_DIM, 2 * KV_NSA * HEAD_DIM,
             3 * H_NSA, D_NSA, 3 * D_FOX, H_FOX, D_FOX)
N_IN = sum(COL_SIZES)
COL_CUTS = tuple(int(c) for c in np.cumsum(COL_SIZES)[:-1])

kernel_name = 'hymba_nsa_fox_decode_step'


def rms_norm(x, g):
    xf = x.astype(jnp.float32)
    y = xf * lax.rsqrt(jnp.mean(xf * xf, axis=-1, keepdims=True) + RMS_EPS)
    return (y * g.astype(jnp.float32)).astype(x.dtype)


def masked_softmax(logits, mask):
    l = jnp.where(mask, logits.astype(jnp.float32), NEG_INF)
    m = jnp.max(l, axis=-1, keepdims=True)
    e = jnp.where(mask, jnp.exp(l - m), 0.0)
    return e / jnp.maximum(jnp.sum(e, axis=-1, keepdims=True), 1e-30)


def t5_bucket(rel):
    n = jnp.maximum(rel, 0)
    nf = jnp.maximum(n, T5_EXACT).astype(jnp.float32)
    large = T5_EXACT + (jnp.log(nf / T5_EXACT) / math.log(T5_MAX_DIST / T5_EXACT)
                        * (T5_BUCKETS - T5_EXACT)).astype(jnp.int32)
    return jnp.where(n < T5_EXACT, n, jnp.minimum(large, T5_BUCKETS - 1))


def gather_pages(cache, page_table, layer):
    rows = cache[page_table, layer]
    b, n_pages, page = rows.shape[:3]
    return rows.reshape((b, n_pages * page) + rows.shape[3:])


def project(h, w_in, b_fgate):
    b, t = h.shape[:2]
    u = h @ w_in
    q_n, kv_c, kv_s, kv_w, g_n, z_n, qkv_f, f_f, z_f = jnp.split(u, COL_CUTS, axis=-1)
    kv_shape = (b, t, 2, KV_NSA, HEAD_DIM)
    qkv_f = qkv_f.reshape(b, t, 3, H_FOX, HEAD_DIM)
    logf = jax.nn.log_sigmoid(f_f.astype(jnp.float32) + b_fgate.astype(jnp.float32))
    gates = jax.nn.sigmoid(g_n.reshape(b, t, KV_NSA, GRP, 3))
    return (q_n.reshape(b, t, KV_NSA, GRP, HEAD_DIM), kv_c.reshape(kv_shape), kv_s.reshape(kv_shape),
            kv_w.reshape(kv_shape), gates, z_n, qkv_f[:, :, 0], qkv_f[:, :, 1:], logf, z_f)


def compress_blocks(kv, cmp_pos, w_cmp1, w_cmp2):
    b, t = kv.shape[:2]
    n_sub = t // CMP_STRIDE
    sub = kv[:, :n_sub * CMP_STRIDE].reshape(b, n_sub, CMP_STRIDE, 2, KV_NSA, HEAD_DIM)
    blk = jnp.concatenate([sub[:, :-1], sub[:, 1:]], axis=2)
    blk = blk + cmp_pos.transpose(1, 0, 2)[:, :, None, :]
    flat = blk.transpose(0, 1, 3, 4, 2, 5).reshape(b, n_sub - 1, 2, KV_NSA, CMP_LEN * HEAD_DIM)
    hid = jax.nn.silu(jnp.einsum('bnekx,exf->bnekf', flat, w_cmp1))
    return jnp.einsum('bnekf,efd->bnekd', hid, w_cmp2)


def selection_blocks(kv):
    b, t = kv.shape[:2]
    n_sel = -(-t // SEL_BLOCK)
    kv = jnp.pad(kv, ((0, 0), (0, n_sel * SEL_BLOCK - t), (0, 0), (0, 0), (0, 0)))
    kv = kv.reshape(b, n_sel, SEL_BLOCK, 2, KV_NSA, HEAD_DIM).transpose(0, 4, 1, 2, 3, 5)
    return kv[..., 0, :], kv[..., 1, :]


def nsa_attend(q, q_pos, cmp_kv, slc_k, slc_v, win_kv, win_pos, gates, t5_table):
    b, tq = q.shape[:2]
    scale = HEAD_DIM ** -0.5
    tbl = t5_table.reshape(T5_BUCKETS, KV_NSA, GRP)
    n_cmp = cmp_kv.shape[1]
    end_pos = jnp.arange(n_cmp, dtype=jnp.int32) * CMP_STRIDE + CMP_LEN - 1
    rel_c = q_pos[:, None] - end_pos[None, :]
    bias_c = tbl[t5_bucket(rel_c)].astype(jnp.float32).transpose(0, 2, 3, 1)
    logit_c = jnp.einsum('bqghd,bngd->bqghn', q, cmp_kv[:, :, 0]).astype(jnp.float32) * scale + bias_c
    p_c = masked_softmax(logit_c, (rel_c >= 0)[:, None, None, :])
    o_c = jnp.einsum('bqghn,bngd->bqghd', p_c.astype(q.dtype), cmp_kv[:, :, 1])
    imp = p_c.sum(axis=3)
    pp = jnp.pad(imp, ((0, 0), (0, 0), (0, 0), (1, 1)))
    p_sub = pp[..., 1:] + pp[..., :-1]
    n_sel = slc_k.shape[2]
    p_sub = jnp.pad(p_sub, ((0, 0), (0, 0), (0, 0), (0, n_sel * SUB_PER_SEL - p_sub.shape[-1])))
    imp_sel = p_sub.reshape(p_sub.shape[:-1] + (n_sel, SUB_PER_SEL)).sum(-1)
    blk = jnp.arange(n_sel, dtype=jnp.int32)[None, :]
    cur = (q_pos // SEL_BLOCK)[:, None]
    forced = (blk == 0) | (blk == cur) | (blk == cur - 1)
    future = blk * SEL_BLOCK > q_pos[:, None]
    score = jnp.where(future[:, None], NEG_INF, jnp.where(forced[:, None], FORCED_SCORE, imp_sel))
    n_top = min(TOP_N, n_sel)
    _, idx = lax.top_k(score, n_top)
    idx = idx.transpose(0, 2, 1, 3)
    gather = jax.vmap(jax.vmap(lambda blocks, ib: blocks[ib]))
    k_sel = gather(slc_k, idx)
    v_sel = gather(slc_v, idx)
    pos_sel = idx[..., None] * SEL_BLOCK + jnp.arange(SEL_BLOCK, dtype=jnp.int32)
    rel_s = q_pos[None, None, :, None, None] - pos_sel
    bias_s = jax.vmap(lambda bk, tb: tb[bk], in_axes=(1, 1), out_axes=1)(t5_bucket(rel_s), tbl)
    bias_s = bias_s.astype(jnp.float32).transpose(0, 2, 1, 5, 3, 4).reshape(b, tq, KV_NSA, GRP, n_top * SEL_BLOCK)
    logit_s = jnp.einsum('bqghd,bgqncd->bqghnc', q, k_sel).reshape(b, tq, KV_NSA, GRP, n_top * SEL_BLOCK)
    logit_s = logit_s.astype(jnp.float32) * scale + bias_s
    mask_s = (rel_s >= 0).transpose(0, 2, 1, 3, 4).reshape(b, tq, KV_NSA, 1, n_top * SEL_BLOCK)
    p_s = masked_softmax(logit_s, mask_s).reshape(b, tq, KV_NSA, GRP, n_top, SEL_BLOCK)
    o_s = jnp.einsum('bqghnc,bgqncd->bqghd', p_s.astype(q.dtype), v_sel)
    rel_w = q_pos[:, None] - win_pos[None, :]
    bias_w = tbl[t5_bucket(rel_w)].astype(jnp.float32).transpose(0, 2, 3, 1)
    mask_w = ((rel_w >= 0) & (rel_w < WINDOW) & (win_pos[None, :] >= 0))[:, None, None, :]
    logit_w = jnp.einsum('bqghd,bsgd->bqghs', q, win_kv[:, :, 0]).astype(jnp.float32) * scale + bias_w
    p_w = masked_softmax(logit_w, mask_w)
    o_w = jnp.einsum('bqghs,bsgd->bqghd', p_w.astype(q.dtype), win_kv[:, :, 1])
    return gates[..., 0:1] * o_c + gates[..., 1:2] * o_s + gates[..., 2:3] * o_w


def fox_attend(q, cq, q_pos, k, v, ck, k_pos):
    logit = jnp.einsum('bqhd,bshd->bhqs', q, k).astype(jnp.float32) * (HEAD_DIM ** -0.5)
    logit = logit + (cq.transpose(0, 2, 1)[..., :, None] - ck.transpose(0, 2, 1)[..., None, :])
    p = masked_softmax(logit, k_pos[None, :] <= q_pos[:, None])
    return jnp.einsum('bhqs,bshd->bqhd', p.astype(q.dtype), v)


def mixer_out(x, o_n, z_n, o_f, z_f, g_post, w_out, p, w_pproj, w_pgate):
    b, t = x.shape[:2]
    mix = jnp.concatenate([o_n.reshape(b, t, D_NSA) * jax.nn.silu(z_n),
                           o_f.reshape(b, t, D_FOX) * jax.nn.silu(z_f)], axis=-1)
    x = x + rms_norm(mix @ w_out, g_post)
    return x + (p @ w_pproj) * jax.nn.sigmoid(x @ w_pgate)


def prompt_layer(x, p, g_pre, g_post, w_in, b_fgate, cmp_pos, w_cmp1, w_cmp2, w_out, w_pproj, w_pgate, t5_table):
    b, t = x.shape[:2]
    n_blk = t // Q_BLOCK
    q_n, kv_c, kv_s, kv_w, gates, z_n, q_f, kv_f, logf, z_f = project(rms_norm(x, g_pre), w_in, b_fgate)
    pos = jnp.arange(t, dtype=jnp.int32)
    cmp_kv = compress_blocks(kv_c, cmp_pos, w_cmp1, w_cmp2)
    slc_k, slc_v = selection_blocks(kv_s)
    win_pad = jnp.pad(kv_w, ((0, 0), (WINDOW, 0), (0, 0), (0, 0), (0, 0)))

    def nsa_block(args):
        i, qb, gb = args
        q0 = i * Q_BLOCK
        wkv = lax.dynamic_slice_in_dim(win_pad, q0, WINDOW + Q_BLOCK, axis=1)
        wpos = q0 - WINDOW + jnp.arange(WINDOW + Q_BLOCK, dtype=jnp.int32)
        qpos = q0 + jnp.arange(Q_BLOCK, dtype=jnp.int32)
        return nsa_attend(qb, qpos, cmp_kv, slc_k, slc_v, wkv, wpos, gb, t5_table)

    to_blocks = lambda a: a.reshape((b, n_blk, Q_BLOCK) + a.shape[2:]).swapaxes(0, 1)
    from_blocks = lambda a: a.swapaxes(0, 1).reshape((b, t) + a.shape[3:])
    o_n = from_blocks(lax.map(nsa_block, (jnp.arange(n_blk, dtype=jnp.int32), to_blocks(q_n), to_blocks(gates))))

    c = jnp.cumsum(logf, axis=1)
    k_f, v_f = kv_f[:, :, 0], kv_f[:, :, 1]

    def fox_block(args):
        qb, cb, pb = args
        return fox_attend(qb, cb, pb, k_f, v_f, c, pos)

    o_f = from_blocks(lax.map(fox_block, (to_blocks(q_f), to_blocks(c), pos.reshape(n_blk, Q_BLOCK))))
    y = mixer_out(x, o_n, z_n, o_f, z_f, g_post, w_out, p, w_pproj, w_pgate)
    w_keep = min(WINDOW, t)
    return y, (kv_c, kv_s, kv_f, logf, kv_w[:, t - w_keep:])


def sample_layer(x, p, cache_cmp_kv, cache_slc_kv, cache_fox_kv, cache_fox_logf, state_win_kv, page_table, layer,
                 g_pre, g_post, w_in, b_fgate, cmp_pos, w_cmp1, w_cmp2, w_out, w_pproj, w_pgate, t5_table):
    b, t = x.shape[:2]
    past_len = page_table.shape[1] * PAGE_SIZE
    q_n, kv_c, kv_s, kv_w, gates, z_n, q_f, kv_f, logf, z_f = project(rms_norm(x, g_pre), w_in, b_fgate)
    pos = past_len + jnp.arange(t, dtype=jnp.int32)
    full_c = jnp.concatenate([gather_pages(cache_cmp_kv, page_table, layer), kv_c], axis=1)
    cmp_kv = compress_blocks(full_c, cmp_pos, w_cmp1, w_cmp2)
    full_s = jnp.concatenate([gather_pages(cache_slc_kv, page_table, layer), kv_s], axis=1)
    slc_k, slc_v = selection_blocks(full_s)
    win_kv = jnp.concatenate([state_win_kv[:, layer], kv_w], axis=1)
    w_buf = state_win_kv.shape[2]
    win_pos = past_len - w_buf + jnp.arange(w_buf + t, dtype=jnp.int32)
    o_n = nsa_attend(q_n, pos, cmp_kv, slc_k, slc_v, win_kv, win_pos, gates, t5_table)
    past_kv = gather_pages(cache_fox_kv, page_table, layer)
    c_past = jnp.cumsum(gather_pages(cache_fox_logf, page_table, layer).astype(jnp.float32), axis=1)
    c_new = c_past[:, -1:] + jnp.cumsum(logf, axis=1)
    k_all = jnp.concatenate([past_kv[:, :, 0], kv_f[:, :, 0]], axis=1)
    v_all = jnp.concatenate([past_kv[:, :, 1], kv_f[:, :, 1]], axis=1)
    o_f = fox_attend(q_f, c_new, pos, k_all, v_all, jnp.concatenate([c_past, c_new], axis=1),
                     jnp.arange(past_len + t, dtype=jnp.int32))
    y = mixer_out(x, o_n, z_n, o_f, z_f, g_post, w_out, p, w_pproj, w_pgate)
    w_keep = min(WINDOW, past_len + t)
    return y, (kv_c, kv_s, kv_f, logf, win_kv[:, win_kv.shape[1] - w_keep:])


def setup_inputs(seed: int = 0) -> dict:
    key = jax.random.key(seed)
    ks = jax.random.split(key, 24)
    n_pages = PAST_LEN // PAGE_SIZE
    n_used = DEC_BATCH * n_pages
    n_pool = n_used + (n_used + 3) // 4
    w_buf = min(WINDOW, PAST_LEN)
    nrm = lambda k, shape, s=1.0: s * jax.random.normal(k, shape, jnp.float32)
    page_table = jax.random.permutation(ks[0], n_pool)[:n_used].reshape(DEC_BATCH, n_pages).astype(jnp.int32)
    return {
        'x_prompt': nrm(ks[1], (BATCH, SEQ, D_MODEL)),
        'x_sample': nrm(ks[2], (DEC_BATCH, DEC_SEQ, D_MODEL)),
        'cache_cmp_kv': nrm(ks[3], (n_pool, DEPTH, PAGE_SIZE, 2, KV_NSA, HEAD_DIM)),
        'cache_slc_kv': nrm(ks[4], (n_pool, DEPTH, PAGE_SIZE, 2, KV_NSA, HEAD_DIM)),
        'cache_fox_kv': nrm(ks[5], (n_pool, DEPTH, PAGE_SIZE, 2, H_FOX, HEAD_DIM)),
        'cache_fox_logf': jax.nn.log_sigmoid(FGATE_BIAS_INIT + nrm(ks[6], (n_pool, DEPTH, PAGE_SIZE, H_FOX))),
        'state_win_kv': nrm(ks[7], (DEC_BATCH, DEPTH, w_buf, 2, KV_NSA, HEAD_DIM)),
        'page_table': page_table,
        'p_prompt': nrm(ks[8], (DEPTH, BATCH, SEQ, PLE_DIM)),
        'p_sample': nrm(ks[9], (DEPTH, DEC_BATCH, DEC_SEQ, PLE_DIM)),
        'g_pre': 1.0 + nrm(ks[10], (DEPTH, D_MODEL), 0.05),
        'g_post': 1.0 + nrm(ks[11], (DEPTH, D_MODEL), 0.05),
        'w_in': nrm(ks[12], (DEPTH, D_MODEL, N_IN), D_MODEL ** -0.5),
        'b_fgate': FGATE_BIAS_INIT + nrm(ks[13], (DEPTH, H_FOX), 0.5),
        'cmp_pos': nrm(ks[14], (DEPTH, 2, CMP_LEN, HEAD_DIM), 0.1),
        'w_cmp1': nrm(ks[15], (DEPTH, 2, CMP_LEN * HEAD_DIM, CMP_HIDDEN), (CMP_LEN * HEAD_DIM) ** -0.5),
        'w_cmp2': nrm(ks[16], (DEPTH, 2, CMP_HIDDEN, HEAD_DIM), CMP_HIDDEN ** -0.5),
        'w_out': nrm(ks[17], (DEPTH, D_MODEL, D_MODEL), D_MODEL ** -0.5),
        'w_pproj': nrm(ks[18], (DEPTH, PLE_DIM, D_MODEL), PLE_DIM ** -0.5),
        'w_pgate': nrm(ks[19], (DEPTH, D_MODEL, D_MODEL), D_MODEL ** -0.5),
        't5_table': nrm(ks[20], (T5_BUCKETS, H_NSA), 0.5),
    }


def reference(x_prompt, x_sample, cache_cmp_kv, cache_slc_kv, cache_fox_kv, cache_fox_logf, state_win_kv,
              page_table, p_prompt, p_sample, g_pre, g_post, w_in, b_fgate, cmp_pos, w_cmp1, w_cmp2,
              w_out, w_pproj, w_pgate, t5_table):
    xp, xs = x_prompt, x_sample
    rows_p, rows_s = [], []
    for layer in range(DEPTH):
        lw = (g_pre[layer], g_post[layer], w_in[layer], b_fgate[layer], cmp_pos[layer], w_cmp1[layer],
              w_cmp2[layer], w_out[layer], w_pproj[layer], w_pgate[layer], t5_table)
        xp, r_p = prompt_layer(xp, p_prompt[layer], *lw)
        xs, r_s = sample_layer(xs, p_sample[layer], cache_cmp_kv, cache_slc_kv, cache_fox_kv, cache_fox_logf,
                               state_win_kv, page_table, layer, *lw)
        rows_p.append(r_p)
        rows_s.append(r_s)
    stack = lambda rows, i: jnp.stack([r[i] for r in rows], axis=1)
    return (xp, xs,
            stack(rows_p, 0), stack(rows_p, 1), stack(rows_p, 2), stack(rows_p, 3), stack(rows_p, 4),
            stack(rows_s, 0), stack(rows_s, 1), stack(rows_s, 2), stack(rows_s, 3), stack(rows_s, 4))
```

```python
import os
import numpy as np
from contextlib import ExitStack
import ml_dtypes
import concourse.bass as bass
import concourse.mybir as mybir
from concourse.bass_utils import run_bass_kernel_spmd

F32 = mybir.dt.float32
BF16 = mybir.dt.bfloat16
I32 = mybir.dt.int32
ALU = mybir.AluOpType
AF = mybir.ActivationFunctionType
AX = mybir.AxisListType

D = 1024
NIN = 3872
NB_P = 16
NS = 4
NBLK = NB_P + NS
C_KVC, C_KVS, C_KVW, C_G, C_ZN = 512, 768, 1024, 1280, 1304
C_QF, C_KF, C_VF, C_FF, C_ZF = 1816, 2328, 2840, 3352, 3360
RMS_EPS = 1e-6


class Buf:
    def __init__(self, name):
        self.name = name
        self.last_w = None
        self.reads = []
        self.wsem = None
        self.wcount = 0


class Eng:
    def __init__(self, name, sem):
        self.name = name
        self.sem = sem
        self.count = 0
        self.seen = {}
        self.ops = []
        self.pending = []


class Sched:
    def __init__(self, nc, stack):
        self.nc = nc
        self.stack = stack
        self.nsem = 0
        self.dma_hist = {}
        self.max_inflight = int(os.environ.get("KINFL", "6"))
        self.eng = {}
        for n in ("pe", "act", "dve", "pool", "sp"):
            self.eng[n] = Eng(n, stack.enter_context(nc.semaphore("sem_" + n)))
        self.bufs = {}

    def buf(self, name):
        if name not in self.bufs:
            self.bufs[name] = Buf(name)
        return self.bufs[name]

    def _waits(self, e, reads, writes):
        w = {}

        def add(st):
            if st is None:
                return
            if st == "PENDING":
                assert e.name == "pe", "dependency on a pending (non-incrementing) op"
                return
            sem, val = st
            key = id(sem)
            if key not in w or w[key][1] < val:
                w[key] = (sem, val)

        for b in reads:
            add(b.last_w)
        for b in writes:
            add(b.last_w)
            for r in b.reads:
                add(r)
        out = []
        for key, (sem, val) in w.items():
            if e.name == "pe" and sem is e.sem:
                continue
            if e.seen.get(key, 0) >= val:
                continue
            e.seen[key] = val
            out.append((sem, val))
        return out

    def op(self, engname, fn, reads=(), writes=(), inc=True):
        e = self.eng[engname]
        reads = [self.buf(b) if isinstance(b, str) else b for b in reads]
        writes = [self.buf(b) if isinstance(b, str) else b for b in writes]
        waits = self._waits(e, reads, writes)
        sem = e.sem

        def run(h, waits=waits, fn=fn, inc=inc, sem=sem):
            for (s, v) in waits:
                h.wait_ge(s, v)
            ins = fn(h)
            if inc:
                ins.then_inc(sem, 1)

        e.ops.append(run)
        if inc:
            e.count += 1
            st = (sem, e.count)
            for (b, kind) in e.pending:
                if kind == "w":
                    b.last_w = st
                    b.reads = []
                else:
                    b.reads.append(st)
            e.pending = []
            for b in writes:
                b.last_w = st
                b.reads = []
            for b in reads:
                b.reads.append(st)
        else:
            for b in writes:
                b.last_w = "PENDING"
                e.pending.append((b, "w"))
            for b in reads:
                e.pending.append((b, "r"))

    def dma(self, engname, out, in_, reads=(), writes=(), indirect=None):
        e = self.eng[engname]
        reads = [self.buf(b) if isinstance(b, str) else b for b in reads]
        writes = [self.buf(b) if isinstance(b, str) else b for b in writes]
        assert len(writes) == 1
        dst = writes[0]
        waits = self._waits(e, reads, writes)
        hist = self.dma_hist.setdefault(engname, [])
        if len(hist) >= self.max_inflight:
            sem_o, val_o = hist[-self.max_inflight]
            if e.seen.get(id(sem_o), 0) < val_o:
                e.seen[id(sem_o)] = val_o
                waits.append((sem_o, val_o))
        if dst.wsem is None:
            dst.wsem = self.stack.enter_context(self.nc.semaphore("dsem%d" % self.nsem))
            self.nsem += 1
        dst.wcount += 16
        sem, val = dst.wsem, dst.wcount

        def run(h, waits=waits, sem=sem):
            for (s, v) in waits:
                h.wait_ge(s, v)
            if indirect is None:
                h.dma_start(out=out, in_=in_).then_inc(sem, 16)
            else:
                h.indirect_dma_start(out=out, out_offset=None, in_=in_, in_offset=indirect).then_inc(sem, 16)

        e.ops.append(run)
        st = (sem, val)
        hist.append(st)
        dst.last_w = st
        dst.reads = []
        for b in reads:
            b.reads.append(st)

    def wait_all(self, engname, bufs):
        e = self.eng[engname]
        bufs = [self.buf(b) if isinstance(b, str) else b for b in bufs]
        waits = self._waits(e, bufs, [])

        def run(h, waits=waits):
            for (s, v) in waits:
                h.wait_ge(s, v)

        e.ops.append(run)

    def barrier(self):
        stamps = [(e.sem, e.count) for e in self.eng.values() if e.count > 0]
        stamps += [(b.wsem, b.wcount) for b in self.bufs.values() if b.wsem is not None]
        for e in self.eng.values():
            waits = []
            for (sem, val) in stamps:
                if sem is e.sem:
                    continue
                if e.seen.get(id(sem), 0) >= val:
                    continue
                e.seen[id(sem)] = val
                waits.append((sem, val))

            def run(h, waits=waits):
                for (s_, v) in waits:
                    h.wait_ge(s_, v)

            e.ops.append(run)

    def emit(self, block):
        m = {"pe": block.tensor, "act": block.scalar, "dve": block.vector, "pool": block.gpsimd, "sp": block.sync}
        for n, deco in m.items():
            ops = self.eng[n].ops

            def body(h, ops=ops):
                for f in ops:
                    f(h)

            deco(body)


NT = 65
NSEQ = 1 + NS
NEG = -30000.0
NPOOL = int(os.environ.get("KPOOL", "2560"))


_PACK_SPECS = [
    ("xsamp", (NS, 128, D), "f"), ("valid", (128, 64), "f"), ("pown", (NBLK, 128, 256), "f"), ("w_in", (D, NIN), "f"),
    ("g_pre", (1, D), "f"), ("g_post", (1, D), "f"), ("b_fg", (1, 8), "f"), ("swin", (NS, 512, 256), "f"),
    ("tri", (128, 128), "f"), ("lastsel", (128, 128), "f"), ("w_out", (D, D), "f"), ("w_pgate", (D, D), "f"),
    ("w_pproj", (256, D), "f"), ("ohw", (33, 768), "f"), ("tblx", (33, 8), "f"), ("ohs_s", (33, 1536), "f"),
    ("ohs_c", (33, 16384), "f"), ("w_cmp1", (2, 2048, 128), "f"), ("w_cmp2", (2, 128, 64), "f"), ("cmp_pos", (2, 32, 64), "f"),
    ("keep", (NBLK, 128, 128), "f"), ("addc", (NBLK, 128, 128), "f"), ("vbc", (128, 2, 4), "f"),
    ("ident", (128, 128), "b"), ("diagmask", (128, 128), "b"), ("jmat", (128, 128), "b"), ("mmat", (128, 4, 128), "b"),
    ("ehalf", (128, NT, 2), "b"),
]


def _pack_layout():
    lay, tot = {}, {"f": 0, "b": 0}
    for name, shape, k in _PACK_SPECS:
        n = int(np.prod(shape))
        lay[name] = (k, tot[k], tuple(shape))
        tot[k] += (n + 63) // 64 * 64
    return lay, tot


def build_nc():
    nc = bass.Bass("TRN2", target_bir_lowering=False)
    stack = ExitStack()
    lay, tot = _pack_layout()
    packs = {"f": nc.dram_tensor("packf", [tot["f"]], F32, kind="ExternalInput").ap(),
             "b": nc.dram_tensor("packb", [tot["b"]], BF16, kind="ExternalInput").ap()}

    def din(name, shape, dt=F32):
        if name in lay:
            k, off, shp = lay[name]
            assert tuple(shape) == shp and (dt == F32) == (k == "f"), name
            n = int(np.prod(shp))
            names = ["d%d" % i for i in range(len(shp))]
            pat = "(" + " ".join(names) + ") -> " + " ".join(names)
            return packs[k][off:off + n].rearrange(pat, **{nm: int(v) for nm, v in zip(names[1:], shp[1:])})
        return nc.dram_tensor(name, list(shape), dt, kind="ExternalInput").ap()

    def dout(name, shape, dt=F32):
        return nc.dram_tensor(name, list(shape), dt, kind="ExternalOutput").ap()

    def dscr(name, shape, dt=F32):
        return nc.dram_tensor(name, list(shape), dt).ap()

    xv = din("xv", [64, 128, D])
    xsamp = din("xsamp", [NS, 128, D])
    valid = din("valid", [128, 64])
    pown = din("pown", [NBLK, 128, 256])
    w_in = din("w_in", [D, NIN])
    g_pre = din("g_pre", [1, D])
    g_post = din("g_post", [1, D])
    b_fg = din("b_fg", [1, 8])
    swin = din("swin", [NS, 512, 256])
    identd = din("ident", [128, 128], BF16)
    trid = din("tri", [128, 128])
    lastd = din("lastsel", [128, 128])
    diagd = din("diagmask", [128, 128], BF16)
    c_fox = din("c_fox", [NPOOL * 128, 1024])
    c_lf = din("c_lf", [NPOOL * 128, 8])
    ptab = din("ptab", [NS, 64], I32)
    w_out = din("w_out", [D, D])
    w_pgate = din("w_pgate", [D, D])
    w_pproj = din("w_pproj", [256, D])
    RW = 768
    ohw = din("ohw", [33, RW])
    tblx = din("tblx", [33, 8])
    jd = din("jmat", [128, 128], BF16)
    RS, RC, OFFC = 1536, 16384, 7936
    ohsd = din("ohs_s", [33, RS])
    ohcd = din("ohs_c", [33, RC])
    c_cmp = din("c_cmp", [NPOOL * 128, 256])
    c_slc = din("c_slc", [NPOOL * 128, 256])
    w_cmp1 = din("w_cmp1", [2, 2048, 128])
    w_cmp2 = din("w_cmp2", [2, 128, 64])
    cmp_pos = din("cmp_pos", [2, 32, 64])
    mmatd = din("mmat", [128, 4, 128], BF16)
    ehd = din("ehalf", [128, NT, 2], BF16)
    keepd = din("keep", [NBLK, 128, 128])
    addcd = din("addc", [NBLK, 128, 128])
    vbcd = din("vbc", [128, 2, 4])

    o_kvc = dout("o_kvc", [NBLK, 128, 256])
    o_kvs = dout("o_kvs", [NBLK, 128, 256])
    o_kvw = dout("o_kvw", [NBLK, 128, 256])
    o_kvf = dout("o_kvf", [NBLK, 128, 1024])
    o_logf = dout("o_logf", [NBLK, 128, 8])
    o_wins = dout("o_wins", [NS, 512, 256])
    o_y = dout("o_y", [NBLK, 128, D])

    kfT = dscr("kfT", [NSEQ, 8, 70, NT * 128], BF16)
    vfd = dscr("vfd", [NSEQ, 8, NT, 128, 65], BF16)
    uscr = dscr("uscr", [NBLK, 128, 1048])
    qfTs = dscr("qfTs", [NBLK, 70, 8, 128], BF16)
    kwT = dscr("kwT", [NSEQ, 128, NT * 128], BF16)
    ksT = dscr("ksT", [NSEQ, 128, NT * 128], BF16)
    vsd = dscr("vsd", [NSEQ, NT, 128, 2, 65], BF16)
    kcT = dscr("kcT", [NSEQ, 2, 128, 64 * 128], BF16)
    kcmp_d = dscr("kcmp_d", [NSEQ, 128, 512], BF16)
    vcmp_d = dscr("vcmp_d", [NSEQ, 128, 4, 2, 65], BF16)
    ofd = dscr("ofd", [NBLK, 128, 512], BF16)
    fs_h = nc.dram_tensor("fstab", [8, RS], BF16)
    fc_h = nc.dram_tensor("fctab", [8, RC], BF16)
    vwd = dscr("vwd", [NSEQ, NT, 128, 2, 65], BF16)
    qnTs = dscr("qnTs", [NBLK, 128, 4, 128], BF16)
    fw_h = nc.dram_tensor("fwtab", [8, RW], BF16)
    fwd = fw_h.ap()

    S = Sched(nc, stack)
    stA = stack
    KA0 = os.environ.get("KA", "ts")

    def sb(name, shape, dt=F32, st=None):
        return (st or stack).enter_context(nc.sbuf_tensor(name, list(shape), dt))

    def pst(name, shape, dt=F32):
        return stack.enter_context(nc.psum_tensor(name, list(shape), dt))

    psT = pst("psT", [128, 1024], BF16)
    psT2 = pst("psT2", [128, 1024], BF16)
    psU = [pst("psU%d" % i, [128, 512]) for i in range(4)]
    psS = [pst("psS%d" % i, [128, 512]) for i in range(2)]

    ident = sb("identb", [128, 128], BF16)
    diag = sb("diagb", [128, 128], BF16)
    S.dma("sp", ident[:], identd, writes=["ident"])
    S.dma("sp", diag[:], diagd, writes=["diag"])

    big = sb("big", [128, 8 * NIN], BF16)
    wbf = big[:].rearrange("p (c n) -> p c n", c=8)
    wst = sb("wst", [128, 1024], F32, stA)
    gbc = sb("gbc", [128, D], F32, stA)
    bbc = sb("bbc", [128, 8], F32, stA)
    tri = sb("tri_sb", [128, 128], F32, stA)
    lastsel = sb("lastsel_sb", [128, 128], F32, stA)
    vld = sb("vld", [128, 64], F32, stA)
    vbias = sb("vbias", [128, 64], F32, stA)
    xs = [sb("xs%d" % i, [128, D], F32, stA) for i in range(2)]
    sq = sb("sq", [128, D], F32, stA)
    ssq = sb("ssq", [128, 1], F32, stA)
    rstd = sb("rstd", [128, 1], F32, stA)
    hb = sb("hb", [128, D], BF16, stA)
    hT = sb("hT", [128, 8, 128], BF16, stA)
    u = sb("u", [128, NIN], F32, stA)
    lft = sb("lft", [128, 8], F32, stA)
    lfe = sb("lfe", [128, 8], F32, stA)
    lfo = sb("lfo", [128, 8], F32, stA)
    lfv = sb("lfv", [128, 8], F32, stA)
    cc = [sb("cc%d" % i, [128, 8], F32, stA) for i in range(2)]
    negc = sb("negc", [128, 8, 1], F32, stA)
    r1 = sb("r1", [128, 8, 1], F32, stA)
    kaug = sb("kaug", [128, 8, 70], BF16, stA)
    qaug = sb("qaug", [128, 8, 70], BF16, stA)
    kT_sb = sb("kT_sb", [128, 8, 128], BF16, stA)
    qT_sb = sb("qT_sb", [128, 8, 128], BF16, stA)
    vfa = sb("vfa", [128, 8, 65], BF16, stA)
    ckf = sb("ckf", [128, 1024], F32, stA)
    cks = sb("cks", [128, 256])
    ckc = sb("ckc", [128, 256])
    ohst = [sb("ohst%d" % i, [33, 512]) for i in range(2)]
    fts = [sb("fts%d" % i, [8, 512], BF16) for i in range(2)]
    clf = sb("clf", [128, 8], F32, stA)
    pti = sb("pti", [128, 64], I32, stA)
    ptf = sb("ptf", [128, 64], F32, stA)
    pidx_i = sb("pidx_i", [128, 1], I32, stA)
    pidx_f = sb("pidx_f", [128, 1], F32, stA)
    idxf = sb("idxf", [128, 64], F32, stA)
    idxi = sb("idxi", [128, 64], I32, stA)
    kwb = sb("kwb", [128, 128], BF16)
    kwT_sb = sb("kwT_sb", [128, 128], BF16)
    vwa = sb("vwa", [128, 2, 65], BF16)
    qst = sb("qst", [128, 4, 128], BF16)
    qnT_sb = sb("qnT_sb", [128, 4, 128], BF16)
    swt = sb("swt", [128, 256])
    ohs = sb("ohs", [33, RW])
    tbs = sb("tbs", [33, 8])
    fws = sb("fws", [8, RW], BF16)
    jm = sb("jm", [128, 128], BF16)
    Bw = sb("Bw", [128, 5, 4, 128], BF16)
    S.dma("sp", ohs[:], ohw, writes=["ohs"])
    S.dma("sp", tbs[:], tblx, writes=["tbs"])
    S.dma("sp", jm[:], jd, writes=["jm"])
    S.op("pool", lambda h: h.memset(vwa[:], 1.0), writes=["vwa0"])
    S.op("pe", lambda h: h.matmul(out=psU[0][0:8, 0:512], lhsT=tbs[:], rhs=ohs[:, 0:512], start=True, stop=True),
         reads=["tbs", "ohs"], writes=["psU0"])
    S.op("pe", lambda h: h.matmul(out=psU[1][0:8, 0:256], lhsT=tbs[:], rhs=ohs[:, 512:768], start=True, stop=True),
         reads=["tbs", "ohs"], writes=["psU1"])
    S.op("act", lambda h: h.copy(out=fws[:, 0:512], in_=psU[0][0:8, 0:512]), reads=["psU0"], writes=["fws"])
    S.op("act", lambda h: h.copy(out=fws[:, 512:768], in_=psU[1][0:8, 0:256]), reads=["psU1"], writes=["fws"])
    S.dma("sp", fwd, fws[:], reads=["fws"], writes=["fwd"])
    tcnt = 0
    for (src, dst_h, ncol, dname) in (((ohsd, fs_h, RS, "fsd"), (ohcd, fc_h, RC, "fcd")) if "t" in KA0 else ()):
        for c0 in range(0, ncol, 512):
            k2 = tcnt % 2
            tcnt += 1
            S.dma("sp", ohst[k2][:], src[:, c0:c0 + 512], writes=["ohst%d" % k2])
            S.op("pe", lambda h, k2=k2: h.matmul(out=psU[k2][0:8, 0:512], lhsT=tbs[:], rhs=ohst[k2][:], start=True, stop=True),
                 reads=["tbs", "ohst%d" % k2], writes=["psU%d" % k2])
            S.op("act", lambda h, k2=k2: h.copy(out=fts[k2][:], in_=psU[k2][0:8, 0:512]), reads=["psU%d" % k2], writes=["fts%d" % k2])
            S.dma("sp", dst_h.ap()[:, c0:c0 + 512], fts[k2][:], reads=["fts%d" % k2], writes=[dname])

    S.dma("sp", gbc[:], g_pre.partition_broadcast(128), writes=["gbc"])
    S.dma("sp", bbc[:], b_fg.partition_broadcast(128), writes=["bbc"])
    S.dma("sp", tri[:], trid, writes=["tri"])
    S.dma("sp", lastsel[:], lastd, writes=["lastsel"])
    S.dma("sp", vld[:], valid, writes=["vld"])
    S.op("dve", lambda h: h.tensor_scalar(out=vbias[:], in0=vld[:], scalar1=-1.0, scalar2=-NEG, op0=ALU.add, op1=ALU.mult),
         reads=["vld"], writes=["vbias"])
    kaugB = sb("kaugB", [128, 8, 70], BF16)
    vfaB = sb("vfaB", [128, 8, 65], BF16)
    negcB = sb("negcB", [128, 8, 1])
    r1B = sb("r1B", [128, 8, 1])
    kT_sbB = sb("kT_sbB", [128, 8, 128], BF16)
    kwbB = sb("kwbB", [128, 128], BF16)
    kwT_sbB = sb("kwT_sbB", [128, 128], BF16)
    vwaB = sb("vwaB", [128, 2, 65], BF16)
    kaug2, vfa2, negc2, r12, kT2 = [kaug, kaugB], [vfa, vfaB], [negc, negcB], [r1, r1B], [kT_sb, kT_sbB]
    kwb2, kwT2, vwa2 = [kwb, kwbB], [kwT_sb, kwT_sbB], [vwa, vwaB]
    psTa = [psT, psS[0][:].bitcast(BF16)]
    psT2a = [psT2, psS[1][:].bitcast(BF16)]
    psTn, psT2n = ["psT", "psS0"], ["psT2", "psS1"]
    S.op("pool", lambda h: h.memset(kaug[:], 1.0), writes=["kaug0"])
    S.op("pool", lambda h: h.memset(kaugB[:], 1.0), writes=["kaug1"])
    S.op("pool", lambda h: h.memset(vfaB[:], 1.0), writes=["vfa1"])
    S.op("pool", lambda h: h.memset(vwaB[:], 1.0), writes=["vwa1"])
    S.op("pool", lambda h: h.memset(qaug[:], 1.0), writes=["qaug"])
    S.op("pool", lambda h: h.memset(vfa[:], 1.0), writes=["vfa0"])
    S.op("pool", lambda h: h.iota(pidx_i[:], pattern=[[0, 1]], base=0, channel_multiplier=1), writes=["pidx_i"])
    S.op("pool", lambda h: h.tensor_copy(out=pidx_f[:], in_=pidx_i[:]), reads=["pidx_i"], writes=["pidx_f"])

    def load_weight_bf16(dst3, wd, nchunk, ncols, eng_toggle=[0]):
        wv = wd.rearrange("(c p) n -> c p n", p=128)
        for c in range(nchunk):
            for n0 in range(0, ncols, 1024):
                n1 = min(ncols, n0 + 1024)
                S.dma("sp", wst[:, 0:n1 - n0], wv[c][:, n0:n1], writes=["wst"])
                eng_toggle[0] ^= 1
                if eng_toggle[0]:
                    S.op("act", lambda h, c=c, n0=n0, n1=n1: h.copy(out=dst3[:, c, n0:n1], in_=wst[:, 0:n1 - n0]),
                         reads=["wst"], writes=["wdst"])
                else:
                    S.op("pool", lambda h, c=c, n0=n0, n1=n1: h.tensor_copy(out=dst3[:, c, n0:n1], in_=wst[:, 0:n1 - n0]),
                         reads=["wst"], writes=["wdst"])

    load_weight_bf16(wbf, w_in, 8, NIN)

    for s in range(NS):
        S.dma("pool", o_wins[s, 0:508, :], swin[s, 4:512, :], writes=["o_wins"])

    def project_block(x_ap, xi, full=True):
        xb = "xs%d" % xi
        xt = xs[xi]
        S.dma("sp", xt[:], x_ap, writes=[xb])
        S.op("dve", lambda h: h.tensor_tensor(out=sq[:], in0=xt[:], in1=xt[:], op=ALU.mult), reads=[xb], writes=["sq"])
        S.op("dve", lambda h: h.reduce_sum(out=ssq[:], in_=sq[:], axis=AX.X), reads=["sq"], writes=["ssq"])
        S.op("dve", lambda h: h.tensor_scalar(out=rstd[:], in0=ssq[:], scalar1=1.0 / D, scalar2=RMS_EPS,
                                              op0=ALU.mult, op1=ALU.add), reads=["ssq"], writes=["rstd"])
        S.op("act", lambda h: h.activation(out=rstd[:], in_=rstd[:], func=AF.Sqrt), reads=["rstd"], writes=["rstd"])
        S.op("dve", lambda h: h.reciprocal(out=rstd[:], in_=rstd[:]), reads=["rstd"], writes=["rstd"])
        S.op("dve", lambda h: h.scalar_tensor_tensor(out=hb[:], in0=xt[:], scalar=rstd[:, 0:1], in1=gbc[:],
                                                     op0=ALU.mult, op1=ALU.mult), reads=[xb, "rstd", "gbc"], writes=["hb"])
        for c in range(8):
            S.op("pe", lambda h, c=c: h.transpose(out=psT[:, c * 128:(c + 1) * 128], in_=hb[:, c * 128:(c + 1) * 128],
                                                  identity=ident[:]), reads=["hb", "ident"], writes=["psT"], inc=(c == 7))
        S.op("act", lambda h: h.copy(out=hT[:].rearrange("p c t -> p (c t)"), in_=psT[:]), reads=["psT"], writes=["hT"])
        chunks = ([(c0, min(512, NIN - c0)) for c0 in range(0, NIN, 512)] if full else
                  [(C_KVC, 512), (C_KVC + 512, 256), (C_KF, 512), (C_VF, 512), (C_FF, 8)])
        for half in range(1):
            for bk_i, (c0, wd) in enumerate(chunks):
                bk = bk_i % 4
                for c in range(8):
                    S.op("pe", lambda h, bk=bk, c=c, c0=c0, wd=wd: h.matmul(
                        out=psU[bk][:, 0:wd], lhsT=hT[:, c, :], rhs=wbf[:, c, c0:c0 + wd], start=(c == 0), stop=(c == 7)),
                         reads=["hT", "wdst"], writes=["psU%d" % bk], inc=(c == 7))
                if bk % 2 == 0:
                    S.op("act", lambda h, bk=bk, c0=c0, wd=wd: h.copy(out=u[:, c0:c0 + wd], in_=psU[bk][:, 0:wd]),
                         reads=["psU%d" % bk], writes=["u"])
                else:
                    S.op("dve", lambda h, bk=bk, c0=c0, wd=wd: h.tensor_copy(out=u[:, c0:c0 + wd], in_=psU[bk][:, 0:wd]),
                         reads=["psU%d" % bk], writes=["u"])
        S.op("dve", lambda h: h.tensor_tensor(out=lft[:], in0=u[:, C_FF:C_FF + 8], in1=bbc[:], op=ALU.add),
             reads=["u", "bbc"], writes=["lft"])
        S.op("act", lambda h: h.activation(out=lfe[:], in_=lft[:], func=AF.Exp, scale=-1.0), reads=["lft"], writes=["lfe"])
        S.op("act", lambda h: h.activation(out=lfe[:], in_=lfe[:], func=AF.Ln, bias=1.0), reads=["lfe"], writes=["lfe"])
        S.op("dve", lambda h: h.tensor_scalar(out=lfo[:], in0=lfe[:], scalar1=-1.0, scalar2=None, op0=ALU.mult),
             reads=["lfe"], writes=["lfo"])

    def write_outputs(i):
        S.dma("sp", o_kvc[i], u[:, C_KVC:C_KVC + 256], reads=["u"], writes=["o_kvc"])
        S.dma("sp", o_kvs[i], u[:, C_KVS:C_KVS + 256], reads=["u"], writes=["o_kvs"])
        S.dma("sp", o_kvw[i], u[:, C_KVW:C_KVW + 256], reads=["u"], writes=["o_kvw"])
        S.dma("sp", o_kvf[i], u[:, C_KF:C_KF + 1024], reads=["u"], writes=["o_kvf"])
        S.dma("sp", o_logf[i], lfo[:], reads=["lfo"], writes=["o_logf"])
        S.dma("sp", uscr[i, :, 0:512], u[:, C_ZN:C_ZN + 512], reads=["u"], writes=["uscr"])
        S.dma("sp", uscr[i, :, 512:536], u[:, C_G:C_G + 24], reads=["u"], writes=["uscr"])
        S.dma("sp", uscr[i, :, 536:1048], u[:, C_ZF:C_ZF + 512], reads=["u"], writes=["uscr"])
        if i >= NB_P:
            S.dma("sp", o_wins[i - NB_P, 508:512, :], u[0:4, C_KVW:C_KVW + 256], reads=["u"], writes=["o_wins"])

    def stage_kv(sg, t, kf_ap, vf_ap, lf_ap, srcbufs, ci, own_i):
        pp = t % 2
        kaug_, vfa_, negc_, r1_, kT_, psT_ = kaug2[pp], vfa2[pp], negc2[pp], r12[pp], kT2[pp], psTa[pp]
        nka, nvf, nng, nr1, nkt, npt = "kaug%d" % pp, "vfa%d" % pp, "negc%d" % pp, "r1%d" % pp, "kT_sb%d" % pp, psTn[pp]
        cur, prev = cc[ci], cc[1 - ci]
        S.op("pe", lambda h: h.matmul(out=psU[2][:, 0:8], lhsT=tri[:], rhs=lf_ap, start=True, stop=(t == 0)),
             reads=srcbufs + ["tri"], writes=["psU2"], inc=(t == 0))
        if t > 0:
            S.op("pe", lambda h: h.matmul(out=psU[2][:, 0:8], lhsT=lastsel[:], rhs=prev[:], start=False, stop=True),
                 reads=["cc%d" % (1 - ci), "lastsel"], writes=["psU2"])
        S.op("act", lambda h: h.copy(out=cur[:], in_=psU[2][:, 0:8]), reads=["psU2"], writes=["cc%d" % ci])
        S.op("dve", lambda h: h.tensor_scalar(out=negc_[:, :, 0], in0=psU[2][:, 0:8], scalar1=-1.0, scalar2=None, op0=ALU.mult),
             reads=["psU2"], writes=[nng])
        S.op("dve", lambda h: h.tensor_copy(out=kaug_[:, :, 64:65], in_=negc_[:]), reads=[nng], writes=[nka])
        S.op("dve", lambda h: h.tensor_tensor(out=r1_[:], in0=negc_[:], in1=kaug_[:, :, 64:65], op=ALU.subtract),
             reads=[nng, nka], writes=[nr1])
        S.op("dve", lambda h: h.tensor_copy(out=kaug_[:, :, 65:66], in_=r1_[:]), reads=[nr1], writes=[nka])
        S.op("dve", lambda h: h.tensor_tensor(out=r1_[:], in0=r1_[:], in1=kaug_[:, :, 65:66], op=ALU.subtract),
             reads=[nr1, nka], writes=[nr1])
        S.op("dve", lambda h: h.tensor_copy(out=kaug_[:, :, 66:67], in_=r1_[:]), reads=[nr1], writes=[nka])
        if sg == 0 and t < 3:
            S.op("dve", lambda h: h.tensor_scalar(out=kaug_[:, :, 64:65], in0=kaug_[:, :, 64:65], scalar1=vbias[:, t:t + 1],
                                                  scalar2=None, op0=ALU.add), reads=[nka, "vbias"], writes=[nka])
        S.op("dve", lambda h: h.tensor_copy(out=kaug_[:, :, 0:64], in_=kf_ap.rearrange("p (h d) -> p h d", d=64)),
             reads=srcbufs, writes=[nka])
        S.op("dve", lambda h: h.tensor_copy(out=vfa_[:, :, 0:64], in_=vf_ap.rearrange("p (h d) -> p h d", d=64)),
             reads=srcbufs, writes=[nvf])
        for hh in range(8):
            S.op("pe", lambda h, hh=hh: h.transpose(out=psT_[0:70, hh * 128:(hh + 1) * 128], in_=kaug_[:, hh, :], identity=ident[:]),
                 reads=[nka, "ident"], writes=[npt], inc=(hh == 7))
        S.op("act", lambda h: h.copy(out=kT_[0:70].rearrange("p c t -> p (c t)"), in_=psT_[0:70, :]), reads=[npt], writes=[nkt])
        S.dma("sp", kfT[sg, :, :, t * 128:(t + 1) * 128].rearrange("h r k -> r h k"), kT_[0:70], reads=[nkt], writes=["kfT"])
        S.dma("sp", vfd[sg, :, t].rearrange("h p d -> p h d"), vfa_[:], reads=[nvf], writes=["vfd"])
        if own_i is not None:
            S.op("dve", lambda h: h.tensor_scalar(out=qaug[:, :, 0:64], in0=u[:, C_QF:C_QF + 512].rearrange("p (h d) -> p h d", d=64),
                                                  scalar1=0.125, scalar2=None, op0=ALU.mult), reads=["u"], writes=["qaug"])
            S.op("dve", lambda h: h.tensor_scalar(out=qaug[:, :, 67:70], in0=kaug_[:, :, 64:67], scalar1=-1.0, scalar2=None, op0=ALU.mult),
                 reads=[nka], writes=["qaug"])
            for hh in range(8):
                S.op("pe", lambda h, hh=hh: h.transpose(out=psT2[0:70, hh * 128:(hh + 1) * 128], in_=qaug[:, hh, :], identity=ident[:]),
                     reads=["qaug", "ident"], writes=["psT2"], inc=(hh == 7))
            S.op("act", lambda h: h.copy(out=qT_sb[0:70].rearrange("p c t -> p (c t)"), in_=psT2[0:70, :]), reads=["psT2"], writes=["qT_sb"])
            S.dma("sp", qfTs[own_i], qT_sb[0:70], reads=["qT_sb"], writes=["qfTs"])

    wc = [0]

    def stage_cmp(sg, t, kc_ap, vc_ap, srcbufs):
        for e, ap_ in ((0, kc_ap), (1, vc_ap)):
            wc[0] += 1
            q = wc[0] % 2
            kwb_, kwT_, ps_ = kwb2[q], kwT2[q], psT2a[q]
            nkb, nkT, nps = "kwb%d" % q, "kwT_sb%d" % q, psT2n[q]
            S.op("dve", lambda h, ap_=ap_, kwb_=kwb_: h.tensor_copy(out=kwb_[:], in_=ap_), reads=srcbufs, writes=[nkb])
            S.op("pe", lambda h, kwb_=kwb_, ps_=ps_: h.transpose(out=ps_[:, 0:128], in_=kwb_[:], identity=ident[:]),
                 reads=[nkb, "ident"], writes=[nps])
            S.op("act", lambda h, kwT_=kwT_, ps_=ps_: h.copy(out=kwT_[:], in_=ps_[:, 0:128]), reads=[nps], writes=[nkT])
            S.dma("sp", kcT[sg, e, :, t * 128:(t + 1) * 128], kwT_[:], reads=[nkT], writes=["kcT"])

    def stage_win(sg, t, kw_ap, vw_ap, srcbufs, own_i, kdst=None, vdst=None, kname="kwT", vname="vwd"):
        kdst = kwT if kdst is None else kdst
        vdst = vwd if vdst is None else vdst
        wc[0] += 1
        q = wc[0] % 2
        kwb_, kwT_, vwa_, ps_ = kwb2[q], kwT2[q], vwa2[q], psT2a[q]
        nkb, nkT, nvw, nps = "kwb%d" % q, "kwT_sb%d" % q, "vwa%d" % q, psT2n[q]
        S.op("dve", lambda h: h.tensor_copy(out=kwb_[:], in_=kw_ap), reads=srcbufs, writes=[nkb])
        S.op("dve", lambda h: h.tensor_copy(out=vwa_[:, :, 0:64], in_=vw_ap.rearrange("p (g d) -> p g d", d=64)),
             reads=srcbufs, writes=[nvw])
        S.op("pe", lambda h: h.transpose(out=ps_[:, 0:128], in_=kwb_[:], identity=ident[:]), reads=[nkb, "ident"], writes=[nps])
        S.op("act", lambda h: h.copy(out=kwT_[:], in_=ps_[:, 0:128]), reads=[nps], writes=[nkT])
        S.dma("sp", kdst[sg, :, t * 128:(t + 1) * 128], kwT_[:], reads=[nkT], writes=[kname])
        S.dma("sp", vdst[sg, t], vwa_[:], reads=[nvw], writes=[vname])
        if own_i is not None:
            for g in range(2):
                S.op("dve", lambda h, g=g: h.tensor_scalar(out=qst[:, :, g * 64:(g + 1) * 64],
                                                           in0=u[:, g * 256:(g + 1) * 256].rearrange("p (h d) -> p h d", d=64),
                                                           scalar1=0.125, scalar2=None, op0=ALU.mult), reads=["u"], writes=["qst"])
            for hh in range(4):
                S.op("pe", lambda h, hh=hh: h.transpose(out=psT2[:, hh * 128:(hh + 1) * 128], in_=qst[:, hh, :], identity=ident[:]),
                     reads=["qst", "ident"], writes=["psT2"], inc=(hh == 3))
            S.op("act", lambda h: h.copy(out=qnT_sb[:].rearrange("p c t -> p (c t)"), in_=psT2[:, 0:512]), reads=["psT2"], writes=["qnT_sb"])
            S.dma("sp", qnTs[own_i], qnT_sb[:], reads=["qnT_sb"], writes=["qnTs"])

    for t in range(64):
        project_block(xv[t], t % 2, full=(t % 4 == 3))
        S.op("dve", lambda h, t=t: h.tensor_scalar(out=lfv[:], in0=lfo[:], scalar1=vld[:, t:t + 1], scalar2=None, op0=ALU.mult),
             reads=["lfo", "vld"], writes=["lfv"])
        own_i = (t // 4) if (t % 4 == 3) else None
        if own_i is not None:
            write_outputs(own_i)
        stage_kv(0, t, u[:, C_KF:C_KF + 512], u[:, C_VF:C_VF + 512], lfv[:], ["u", "lfv"], t % 2, own_i)
        stage_win(0, t, u[:, C_KVW:C_KVW + 128], u[:, C_KVW + 128:C_KVW + 256], ["u"], own_i)
        if "s" in KA0 or "a" in KA0:
            stage_win(0, t, u[:, C_KVS:C_KVS + 128], u[:, C_KVS + 128:C_KVS + 256], ["u"], None, ksT, vsd, "ksT", "vsd")
        if "s" in KA0 or "b" in KA0:
            stage_cmp(0, t, u[:, C_KVC:C_KVC + 128], u[:, C_KVC + 128:C_KVC + 256], ["u"])

    for s in range(NS):
        sg = 1 + s
        S.dma("sp", pti[:], ptab[s:s + 1, :].partition_broadcast(128), writes=["pti"])
        S.op("dve", lambda h: h.tensor_copy(out=ptf[:], in_=pti[:]), reads=["pti"], writes=["ptf"])
        S.op("dve", lambda h: h.tensor_scalar(out=idxf[:], in0=ptf[:], scalar1=128.0, scalar2=pidx_f[:, 0:1],
                                              op0=ALU.mult, op1=ALU.add), reads=["ptf", "pidx_f"], writes=["idxf"])
        S.op("dve", lambda h: h.tensor_copy(out=idxi[:], in_=idxf[:]), reads=["idxf"], writes=["idxi"])
        for t in range(64):
            S.dma("pool", ckf[:], c_fox, reads=["idxi", "cks"], writes=["ckf"],
                  indirect=bass.IndirectOffsetOnAxis(ap=idxi[:, t:t + 1], axis=0))
            S.dma("pool", clf[:], c_lf, reads=["idxi", "cks"], writes=["clf"],
                  indirect=bass.IndirectOffsetOnAxis(ap=idxi[:, t:t + 1], axis=0))
            stage_kv(sg, t, ckf[:, 0:512], ckf[:, 512:1024], clf[:], ["ckf", "clf"], t % 2, None)
            if ("s" in KA0 or "c" in KA0 or "d" in KA0) and s < int(os.environ.get("KNS2", "9")):
                S.dma("pool", cks[:], c_slc, reads=["idxi", "ckf", "clf"], writes=["cks"],
                      indirect=bass.IndirectOffsetOnAxis(ap=idxi[:, t:t + 1], axis=0))
                if "d" not in KA0 or "e" in KA0:
                    stage_win(sg, t, cks[:, 0:128], cks[:, 128:256], ["cks"], None, ksT, vsd, "ksT", "vsd")
            if t >= 60:
                S.dma("sp", swt[:], swin[s, (t - 60) * 128:(t - 59) * 128, :], writes=["swt"])
                stage_win(sg, t, swt[:, 0:128], swt[:, 128:256], ["swt"], None)
        for t in range(64 if (("s" in KA0 or "c" in KA0 or "d" in KA0) and s < int(os.environ.get("KNS2", "9"))) else 0):
            S.dma("pool", ckc[:], c_cmp, reads=["idxi"], writes=["ckc"],
                  indirect=bass.IndirectOffsetOnAxis(ap=idxi[:, t:t + 1], axis=0))
            if "d" not in KA0 or "f" in KA0:
                stage_cmp(sg, t, ckc[:, 0:128], ckc[:, 128:256], ["ckc"])
        project_block(xsamp[s], 0)
        write_outputs(NB_P + s)
        stage_kv(sg, 64, u[:, C_KF:C_KF + 512], u[:, C_VF:C_VF + 512], lfo[:], ["u", "lfo"], 0, NB_P + s)
        stage_win(sg, 64, u[:, C_KVW:C_KVW + 128], u[:, C_KVW + 128:C_KVW + 256], ["u"], NB_P + s)
        if "s" in KA0:
            stage_win(sg, 64, u[:, C_KVS:C_KVS + 128], u[:, C_KVS + 128:C_KVS + 256], ["u"], None, ksT, vsd, "ksT", "vsd")

    S.barrier()

    gpo = gbc
    x1 = ckf
    gate = xs[1]
    uo = u
    xt3 = xs[0]
    sz = sq
    mix = hb
    mixT = hT
    wg_bf = sb("wg_bf", [128, 8, D], BF16)
    wp_bf = sb("wp_bf", [128, 2, D], BF16)
    qf_sb = [sb("qf_sb%d" % i, [128, 128], BF16) for i in range(2)]
    pT_sb = [sb("pT_sb%d" % i, [128, 512], BF16) for i in range(2)]
    ofx = [sb("ofx%d" % i, [128, 64], BF16) for i in range(2)]
    of_sb = sb("of_sb", [128, 512], BF16)
    den = sb("den", [128, 1])
    o2 = sb("o2", [128, D])
    ssq3 = sb("ssq3", [128, 1])
    rstd3 = sb("rstd3", [128, 1])
    x1b = sb("x1b", [128, D], BF16)
    pt3 = sb("pt3", [128, 256])
    pb3 = sb("pb3", [128, 256], BF16)
    pT3 = sb("pT3", [128, 2, 128], BF16)
    yt = sb("yt", [128, D])
    qn_sb = sb("qn_sb", [128, 4, 128], BF16)
    kw_sb = sb("kw_sb", [128, 5 * 128], BF16)
    vw_sb = sb("vw_sb", [128, 5, 2, 65], BF16)
    gsig = sb("gsig", [128, 24])
    onsa = sb("onsa", [128, 8, 64])
    denw = sb("denw", [128, 4])
    rg = sb("rg", [128, 4])
    Bs = sb("Bs", [128, 10, 4, 128], BF16)
    mmat = sb("mmat_sb", [128, 4, 128], BF16)
    eh = sb("eh_sb", [128, NT, 2], BF16)
    ekt = [sb("ekt%d" % i, [128, 2, 64], BF16) for i in range(2)]
    keep_sb = sb("keep_sb", [128, 128])
    addc_sb = sb("addc_sb", [128, 128])
    vbc = sb("vbc_sb", [128, 2, 4])
    score = sb("score", [128, 128])
    sc2 = sb("sc2", [128, 128])
    m8a = sb("m8a", [128, 8])
    m8b = sb("m8b", [128, 8])
    mbb = sb("mbb", [128, 128], BF16)
    mbT = sb("mbT", [128, 4, 128], BF16)
    kcmp_sb = sb("kcmp_sb", [128, 512], BF16)
    vcmp_sb = sb("vcmp_sb", [128, 4, 2, 65], BF16)
    posb = sb("posb", [32, 2, 64], BF16)
    posf = sb("posf", [32, 2, 64])
    posT = sb("posT", [64, 2, 32], BF16)
    ce = sb("ce", [128, 2])
    w2f = sb("w2f", [128, 2, 64])
    w2p = sb("w2p", [128, 2, 128], BF16)
    w2v = sb("w2v", [128, 64], BF16)

    def load_weight2(dst3, wd, nchunk, ncols, bufname, tog=[0]):
        wv = wd.rearrange("(c p) n -> c p n", p=128)
        for c in range(nchunk):
            S.dma("sp", wst[:, 0:ncols], wv[c], writes=["wst"])
            tog[0] ^= 1
            if tog[0]:
                S.op("act", lambda h, c=c: h.copy(out=dst3[:, c, :], in_=wst[:, 0:ncols]), reads=["wst"], writes=[bufname])
            else:
                S.op("pool", lambda h, c=c: h.tensor_copy(out=dst3[:, c, :], in_=wst[:, 0:ncols]), reads=["wst"], writes=[bufname])

    load_weight2(wg_bf, w_pgate, 8, D, "wg_bf")
    load_weight2(wp_bf, w_pproj, 2, D, "wp_bf")
    S.dma("sp", gpo[:], g_post.partition_broadcast(128), writes=["gbc"])
    S.dma("sp", mmat[:], mmatd, writes=["mmat"])
    S.dma("sp", eh[:], ehd, writes=["eh"])
    S.dma("sp", vbc[:], vbcd, writes=["vbc"])

    own = [(0, i, 4 * i + 3) for i in range(NB_P)] + [(1 + s, NB_P + s, 64) for s in range(NS)]
    grp = 0
    qcnt = [0]

    o0 = NT * 128
    kf_sb = [big[:, 0:o0]]
    vf_sb = [big[:, o0:o0 + NT * 65].rearrange("p (t d) -> p t d", d=65)]
    ocnt = 0
    for sg in range(NSEQ):
        mine = [(i, qt) for (g, i, qt) in own if g == sg]
        maxqt = max(qt for (_, qt) in mine)
        for hh in range(8):
            kb, vb = "kf_sb0", "vf_sb0"
            ksb, vsb = kf_sb[0], vf_sb[0]
            S.dma("sp", ksb[0:70, 0:(maxqt + 1) * 128], kfT[sg, hh, :, 0:(maxqt + 1) * 128], reads=["kfT"], writes=[kb])
            S.dma("sp", vsb[:, 0:maxqt + 1, :], vfd[sg, hh, 0:maxqt + 1].rearrange("t p d -> p t d"), reads=["vfd"], writes=[vb])
            for (i, qt) in mine:
                qsb, qbn = qf_sb[qcnt[0] % 2], "qf_sb%d" % (qcnt[0] % 2)
                qcnt[0] += 1
                S.dma("sp", qsb[0:70, :], qfTs[i, :, hh, :], reads=["qfTs"], writes=[qbn])
                for g0 in range(0, qt + 1, 4):
                    kts = list(range(g0, min(g0 + 4, qt + 1)))
                    pS, pSn = psS[grp % 2], "psS%d" % (grp % 2)
                    pT, pTn = pT_sb[grp % 2], "pT_sb%d" % (grp % 2)
                    grp += 1
                    for jj, kt in enumerate(kts):
                        last = (kt == kts[-1])
                        S.op("pe", lambda h, jj=jj, kt=kt, pS=pS, ksb=ksb, qsb=qsb, qt=qt: h.matmul(
                            out=pS[:, jj * 128:(jj + 1) * 128], lhsT=ksb[0:70, kt * 128:(kt + 1) * 128],
                            rhs=qsb[0:70, :], start=True, stop=(kt != qt)),
                             reads=[kb, qbn], writes=[pSn], inc=(last and kt != qt))
                        if kt == qt:
                            S.op("pe", lambda h, jj=jj, pS=pS: h.matmul(out=pS[:, jj * 128:(jj + 1) * 128], lhsT=ident[:], rhs=diag[:],
                                                                  start=False, stop=True),
                                 reads=["ident", "diag"], writes=[pSn])
                    n = len(kts)
                    S.op("act", lambda h, pS=pS, pT=pT, n=n: h.activation(out=pT[:, 0:n * 128], in_=pS[:, 0:n * 128], func=AF.Exp),
                         reads=[pSn], writes=[pTn])
                    for jj, kt in enumerate(kts):
                        S.op("pe", lambda h, jj=jj, kt=kt, pT=pT, vsb=vsb, qt=qt: h.matmul(
                            out=psU[3][:, 0:65], lhsT=pT[:, jj * 128:(jj + 1) * 128], rhs=vsb[:, kt, :],
                            start=(kt == 0), stop=(kt == qt)),
                             reads=[pTn, vb], writes=["psU3"], inc=(kt == kts[-1]))
                S.op("dve", lambda h: h.tensor_scalar(out=den[:], in0=psU[3][:, 64:65], scalar1=1e-30, scalar2=None, op0=ALU.max),
                     reads=["psU3"], writes=["den"])
                S.op("dve", lambda h: h.reciprocal(out=den[:], in_=den[:]), reads=["den"], writes=["den"])
                ox, oxn = ofx[ocnt % 2], "ofx%d" % (ocnt % 2)
                ocnt += 1
                S.op("dve", lambda h, ox=ox: h.tensor_scalar(out=ox[:], in0=psU[3][:, 0:64], scalar1=den[:, 0:1],
                                                             scalar2=None, op0=ALU.mult), reads=["psU3", "den"], writes=[oxn])
                S.dma("sp", ofd[i, :, hh * 64:(hh + 1) * 64], ox[:], reads=[oxn], writes=["ofd"])

    S.barrier()

    KSTOP = int(os.environ.get("KSTOP", "9"))
    KA = os.environ.get("KA", "ts")
    XT = big[:, 0:8192]
    w1b = big[:, 8192:16384].rearrange("p (e r f) -> p e r f", e=2, r=32)
    hid = [big[:, 16384 + 512 * k:16384 + 512 * (k + 1)] for k in range(2)]
    if KSTOP >= 3:
        S.op("pool", lambda h: h.memset(big[:, 16384:17408], 0.0), writes=["hid0", "hid1"])
        S.op("pool", lambda h: h.memset(vcmp_sb[:], 1.0), writes=["vcmp_sb"])
        S.op("pool", lambda h: h.memset(w2p[:], 0.0), writes=["w2p"])
        for e in range(2):
            w1v = w_cmp1[e].rearrange("(r d) f -> d r f", d=64)
            for r0 in range(0, 32, 8):
                for half in range(2):
                    S.dma("sp", wst[64 * half:64 * half + 64, 0:1024].rearrange("p (r f) -> p r f", r=8), w1v[:, r0:r0 + 8, :], writes=["wst"])
                S.op("act", lambda h, e=e, r0=r0: h.copy(out=w1b[:, e, r0:r0 + 8, :], in_=wst[:, 0:1024].rearrange("p (r f) -> p r f", r=8)),
                     reads=["wst"], writes=["w1b"])
        S.dma("sp", posf[:], cmp_pos.rearrange("e r d -> r e d"), writes=["posf"])
        S.op("dve", lambda h: h.tensor_copy(out=posb[:], in_=posf[:]), reads=["posf"], writes=["posb"])
        for e in range(2):
            S.op("pe", lambda h, e=e: h.transpose(out=psT2[0:64, e * 32:(e + 1) * 32], in_=posb[:, e, :], identity=ident[0:32, 0:32]),
                 reads=["posb", "ident"], writes=["psT2"], inc=(e == 1))
        S.op("act", lambda h: h.copy(out=posT[:].rearrange("p e r -> p (e r)"), in_=psT2[0:64, 0:64]), reads=["psT2"], writes=["posT"])
        for e in range(2):
            for r in range(32):
                S.op("pe", lambda h, e=e, r=r: h.matmul(out=psU[2][:, e:e + 1], lhsT=w1b[0:64, e, r, :], rhs=posT[:, e, r:r + 1],
                                                        start=(r == 0), stop=(r == 31)),
                     reads=["w1b", "posT"], writes=["psU2"], inc=(r == 31))
        S.op("act", lambda h: h.copy(out=ce[:], in_=psU[2][:, 0:2]), reads=["psU2"], writes=["ce"])
        S.dma("sp", w2f[:], w_cmp2.rearrange("e f d -> f e d"), writes=["w2f"])
        for g in range(2):
            S.op("dve", lambda h, g=g: h.tensor_copy(out=w2p[:, g, 64 * g:64 * g + 64], in_=w2f[:, 0, :]), reads=["w2f"], writes=["w2p"])
        S.op("dve", lambda h: h.tensor_copy(out=w2v[:], in_=w2f[:, 1, :]), reads=["w2f"], writes=["w2v"])
    for sg in range(NSEQ if KSTOP >= 3 else 0):
        for e in range(2):
            S.dma("sp", XT, kcT[sg, e], reads=["kcT"], writes=["XT"])
            for g in range(2):
                for r in range(32):
                    S.op("pe", lambda h, e=e, g=g, r=r: h.matmul(
                        out=psU[g][:, 0:511], lhsT=w1b[64 * g:64 * g + 64, e, r, :], rhs=XT[64 * g:64 * g + 64, r:r + 8161:16],
                        start=(r == 0), stop=(r == 31)), reads=["w1b", "XT"], writes=["psU%d" % g], inc=(r == 31))
                S.op("act", lambda h, e=e, g=g: h.activation(out=hid[g][:, 0:511], in_=psU[g][:, 0:511], func=AF.Silu, bias=ce[:, e:e + 1]),
                     reads=["psU%d" % g, "ce"], writes=["hid%d" % g])
            if e == 0:
                for g in range(2):
                    S.op("pe", lambda h, g=g: h.matmul(out=psU[2][:, 0:512], lhsT=w2p[:, g, :], rhs=hid[g][:, 0:512],
                                                       start=(g == 0), stop=(g == 1)),
                         reads=["w2p", "hid%d" % g], writes=["psU2"], inc=(g == 1))
                S.op("act", lambda h: h.copy(out=kcmp_sb[:], in_=psU[2][:, 0:512]), reads=["psU2"], writes=["kcmp_sb"])
                S.dma("sp", kcmp_d[sg], kcmp_sb[:], reads=["kcmp_sb"], writes=["kcmp_d"])
            else:
                first = True
                for nt in range(4):
                    for g in range(2):
                        S.op("pe", lambda h, nt=nt, g=g, first=first: h.matmul(
                            out=psU[2][:, (nt * 2 + g) * 64:(nt * 2 + g + 1) * 64], lhsT=hid[g][:, nt * 128:(nt + 1) * 128], rhs=w2v[:],
                            start=first, stop=True), reads=["w2v", "hid%d" % g], writes=["psU2"], inc=(nt == 3 and g == 1))
                        first = False
                S.op("act", lambda h: h.copy(out=vcmp_sb[:, :, :, 0:64], in_=psU[2][:, 0:512].rearrange("p (n g d) -> p n g d", n=4, g=2)),
                     reads=["psU2"], writes=["vcmp_sb"])
                S.dma("sp", vcmp_d[sg], vcmp_sb[:], reads=["vcmp_sb"], writes=["vcmp_d"])

    S.barrier()

    a1 = NT * 128
    a2 = a1 + NT * 130
    a3 = a2 + 8 * D
    ks_sb = big[:, 0:a1]
    vs_sb = big[:, a1:a2].rearrange("p (t g d) -> p t g d", g=2, d=65)
    wo_bf = big[:, a2:a3].rearrange("p (c n) -> p c n", c=8)
    Bc = big[:, a3:a3 + 4 * 8 * 128].rearrange("p (n h q) -> p n h q", n=4, h=8)
    load_weight2(wo_bf, w_out, 8, D, "wo_bf")

    def attend_tail(nsteps_is_last, pT, pTn, v_ap, vname, first, last):
        for hh in range(4):
            S.op("pe", lambda h, hh=hh: h.matmul(out=psU[3][:, hh * 65:(hh + 1) * 65], lhsT=pT[:, hh * 128:(hh + 1) * 128], rhs=v_ap,
                                                 start=(first and hh == 0), stop=last),
                 reads=[pTn, vname], writes=["psU3"], inc=(hh == 3))

    def finish_branch(g, col, accumulate):
        S.op("dve", lambda h: h.tensor_scalar(out=denw[:], in0=psU[3][:, 0:260].rearrange("p (h d) -> p h d", d=65)[:, :, 64],
                                              scalar1=1e-30, scalar2=None, op0=ALU.max), reads=["psU3"], writes=["denw"])
        S.op("dve", lambda h: h.reciprocal(out=denw[:], in_=denw[:]), reads=["denw"], writes=["denw"])
        S.op("dve", lambda h: h.tensor_tensor(out=rg[:], in0=denw[:], in1=gsig[:, g * 12 + col:g * 12 + 12:3], op=ALU.mult),
             reads=["denw", "gsig"], writes=["rg"])
        for hh in range(4):
            if accumulate:
                S.op("dve", lambda h, hh=hh: h.scalar_tensor_tensor(
                    out=onsa[:, g * 4 + hh, :], in0=psU[3][:, hh * 65:hh * 65 + 64], scalar=rg[:, hh:hh + 1], in1=onsa[:, g * 4 + hh, :],
                    op0=ALU.mult, op1=ALU.add), reads=["psU3", "rg", "onsa"], writes=["onsa"])
            else:
                S.op("dve", lambda h, hh=hh: h.tensor_scalar(
                    out=onsa[:, g * 4 + hh, :], in0=psU[3][:, hh * 65:hh * 65 + 64], scalar1=rg[:, hh:hh + 1], scalar2=None,
                    op0=ALU.mult), reads=["psU3", "rg"], writes=["onsa"])

    ecnt = 0
    KBR = os.environ.get("KBR", "wcs")
    for (sg, i, qt) in (own if KSTOP >= 4 else []):
        x_ap = xv[qt] if sg == 0 else xsamp[sg - 1]
        sty = 0 if sg == 0 else 1
        S.dma("sp", uo[:, C_ZN:C_ZN + 512], uscr[i, :, 0:512], reads=["uscr"], writes=["u"])
        S.dma("sp", uo[:, C_G:C_G + 24], uscr[i, :, 512:536], reads=["uscr"], writes=["u"])
        S.dma("sp", uo[:, C_ZF:C_ZF + 512], uscr[i, :, 536:1048], reads=["uscr"], writes=["u"])
        S.dma("sp", xt3[:], x_ap, writes=["xs0"])
        S.dma("sp", pt3[:], pown[i], writes=["pt3"])
        S.dma("sp", of_sb[:], ofd[i], reads=["ofd"], writes=["of_sb"])
        S.op("act", lambda h: h.activation(out=sz[:, 0:512], in_=uo[:, C_ZN:C_ZN + 512], func=AF.Silu), reads=["u"], writes=["sq"])
        S.op("act", lambda h: h.activation(out=sz[:, 512:1024], in_=uo[:, C_ZF:C_ZF + 512], func=AF.Silu), reads=["u"], writes=["sq"])
        S.op("act", lambda h: h.activation(out=gsig[:], in_=uo[:, C_G:C_G + 24], func=AF.Sigmoid), reads=["u"], writes=["gsig"])
        S.dma("sp", qn_sb[:], qnTs[i], reads=["qnTs"], writes=["qn_sb"])
        lo = 0 if sg == 0 else 60
        kmin = max(lo, qt - 4)
        nk = qt - kmin + 1
        S.dma("sp", kw_sb[:, 0:nk * 128], kwT[sg, :, kmin * 128:(qt + 1) * 128], reads=["kwT"], writes=["kw_sb"])
        S.dma("sp", vw_sb[:, 0:nk], vwd[sg, kmin:qt + 1].rearrange("t p g d -> p t g d"), reads=["vwd"], writes=["vw_sb"])
        S.dma("sp", kcmp_sb[:], kcmp_d[sg], reads=["kcmp_d"], writes=["kcmp_sb"])
        S.dma("sp", vcmp_sb[:], vcmp_d[sg], reads=["vcmp_d"], writes=["vcmp_sb"])
        for nt in range(4):
            S.dma("sp", Bc[:, nt], bass.AP(tensor=fc_h, offset=OFFC + 128 * qt - 2048 * nt - 2063, ap=[[16, 128], [RC, 8], [1, 128]]),
                  reads=["fcd"], writes=["Bc"])
        S.dma("sp", keep_sb[:], keepd[i], writes=["keep_sb"])
        S.dma("sp", addc_sb[:], addcd[i], writes=["addc_sb"])
        S.dma("sp", ks_sb[:, 0:(qt + 1) * 128], ksT[sg, :, 0:(qt + 1) * 128], reads=["ksT"], writes=["ks_sb"])
        S.dma("sp", vs_sb[:, 0:qt + 1], vsd[sg, 0:qt + 1].rearrange("t p g d -> p t g d"), reads=["vsd"], writes=["vs_sb"])
        for g in range(2):
            for a in range(5):
                S.dma("sp", Bw[:, a], bass.AP(tensor=fw_h, offset=4 * g * RW + 128 * a, ap=[[1, 128], [RW, 4], [1, 128]]), reads=["fwd"], writes=["Bw"])
            for a in range(10):
                S.dma("sp", Bs[:, a], bass.AP(tensor=fs_h, offset=4 * g * RS + 128 * a, ap=[[1, 128], [RS, 4], [1, 128]]), reads=["fsd"], writes=["Bs"])
            qn_g = qn_sb[64 * g:64 * g + 64].rearrange("p h q -> p (h q)")
            for jj in range(nk):
                kt = kmin + jj
                a = qt - kt
                pS, pSn = psS[grp % 2], "psS%d" % (grp % 2)
                pT, pTn = pT_sb[grp % 2], "pT_sb%d" % (grp % 2)
                grp += 1
                S.op("pe", lambda h, g=g, jj=jj, pS=pS, qn_g=qn_g: h.matmul(
                    out=pS[:, :], lhsT=kw_sb[64 * g:64 * g + 64, jj * 128:(jj + 1) * 128], rhs=qn_g, start=True, stop=False),
                     reads=["kw_sb", "qn_sb"], writes=[pSn], inc=False)
                S.op("pe", lambda h, a=a, pS=pS: h.matmul(out=pS[:, :], lhsT=jm[:], rhs=Bw[:, a].rearrange("p h q -> p (h q)"),
                                                          start=False, stop=True), reads=["jm", "Bw"], writes=[pSn])
                if sg == 0 and kt < 3:
                    S.op("act", lambda h, pS=pS, pT=pT, kt=kt: h.activation(out=pT[:, :], in_=pS[:, :], func=AF.Exp, bias=vbias[:, kt:kt + 1]),
                         reads=[pSn, "vbias"], writes=[pTn])
                else:
                    S.op("act", lambda h, pS=pS, pT=pT: h.activation(out=pT[:, :], in_=pS[:, :], func=AF.Exp), reads=[pSn], writes=[pTn])
                attend_tail(None, pT, pTn, vw_sb[:, jj, g, :], "vw_sb", jj == 0, jj == nk - 1)
            finish_branch(g, 2, False)
            for nt in range(4):
                pS, pSn = psS[grp % 2], "psS%d" % (grp % 2)
                pT, pTn = pT_sb[grp % 2], "pT_sb%d" % (grp % 2)
                grp += 1
                S.op("pe", lambda h, g=g, nt=nt, pS=pS, qn_g=qn_g: h.matmul(
                    out=pS[:, :], lhsT=kcmp_sb[64 * g:64 * g + 64, nt * 128:(nt + 1) * 128], rhs=qn_g, start=True, stop=False),
                     reads=["kcmp_sb", "qn_sb"], writes=[pSn], inc=False)
                S.op("pe", lambda h, g=g, nt=nt, pS=pS: h.matmul(out=pS[:, :], lhsT=jm[:], rhs=Bc[:, nt, 4 * g:4 * g + 4, :].rearrange("p h q -> p (h q)"),
                                                                 start=False, stop=True), reads=["jm", "Bc"], writes=[pSn])
                S.op("act", lambda h, pS=pS, pT=pT, nt=nt, sty=sty: h.activation(out=pT[:, :], in_=pS[:, :], func=AF.Exp, bias=vbc[:, sty, nt:nt + 1]),
                     reads=[pSn, "vbc"], writes=[pTn])
                attend_tail(None, pT, pTn, vcmp_sb[:, nt, g, :], "vcmp_sb", nt == 0, nt == 3)
                for hh in range(4):
                    S.op("pe", lambda h, hh=hh, nt=nt, pT=pT: h.matmul(out=psU[2][:, hh * 128:(hh + 1) * 128], lhsT=pT[:, hh * 128:(hh + 1) * 128],
                                                                       rhs=mmat[:, nt, :], start=(nt == 0 and hh == 0), stop=(nt == 3)),
                         reads=[pTn, "mmat"], writes=["psU2"], inc=(hh == 3))
            finish_branch(g, 0, True)
            S.op("dve", lambda h: h.tensor_scalar(out=score[:], in0=psU[2][:, 0:128], scalar1=denw[:, 0:1], scalar2=None, op0=ALU.mult),
                 reads=["psU2", "denw"], writes=["score"])
            for hh in range(1, 4):
                S.op("dve", lambda h, hh=hh: h.scalar_tensor_tensor(out=score[:], in0=psU[2][:, hh * 128:(hh + 1) * 128], scalar=denw[:, hh:hh + 1],
                                                                    in1=score[:], op0=ALU.mult, op1=ALU.add),
                     reads=["psU2", "denw", "score"], writes=["score"])
            S.op("dve", lambda h: h.tensor_tensor(out=score[:], in0=score[:], in1=keep_sb[:], op=ALU.mult), reads=["score", "keep_sb"], writes=["score"])
            S.op("dve", lambda h: h.tensor_tensor(out=score[:], in0=score[:], in1=addc_sb[:], op=ALU.add), reads=["score", "addc_sb"], writes=["score"])
            S.op("dve", lambda h: h.max(out=m8a[:], in_=score[:]), reads=["score"], writes=["m8a"])
            S.op("dve", lambda h: h.match_replace(out=sc2[:], in_to_replace=m8a[:], in_values=score[:], imm_value=-3.0e38),
                 reads=["score", "m8a"], writes=["sc2"])
            S.op("dve", lambda h: h.max(out=m8b[:], in_=sc2[:]), reads=["sc2"], writes=["m8b"])
            kth = 7 if sg == 0 else 6
            S.op("dve", lambda h, kth=kth: h.tensor_scalar(out=sc2[:], in0=score[:], scalar1=m8b[:, kth:kth + 1], scalar2=None, op0=ALU.is_ge),
                 reads=["score", "m8b"], writes=["sc2"])
            S.op("dve", lambda h: h.tensor_scalar(out=mbb[:], in0=sc2[:], scalar1=-1.0, scalar2=-NEG, op0=ALU.add, op1=ALU.mult),
                 reads=["sc2"], writes=["mbb"])
            S.op("pe", lambda h: h.transpose(out=psT2[:, 0:128], in_=mbb[:], identity=ident[:]), reads=["mbb", "ident"], writes=["psT2"])
            for hh in range(4):
                S.op("act", lambda h, hh=hh: h.copy(out=mbT[:, hh, :], in_=psT2[:, 0:128]), reads=["psT2"], writes=["mbT"])
            for kt in range(qt + 1):
                a = min(qt - kt, 9)
                use_sel = not (sg > 0 and kt == 64)
                pS, pSn = psS[grp % 2], "psS%d" % (grp % 2)
                pT, pTn = pT_sb[grp % 2], "pT_sb%d" % (grp % 2)
                grp += 1
                S.op("pe", lambda h, g=g, kt=kt, pS=pS, qn_g=qn_g: h.matmul(
                    out=pS[:, :], lhsT=ks_sb[64 * g:64 * g + 64, kt * 128:(kt + 1) * 128], rhs=qn_g, start=True, stop=False),
                     reads=["ks_sb", "qn_sb"], writes=[pSn], inc=False)
                S.op("pe", lambda h, a=a, pS=pS, use_sel=use_sel: h.matmul(out=pS[:, :], lhsT=jm[:], rhs=Bs[:, a].rearrange("p h q -> p (h q)"),
                                                                           start=False, stop=(not use_sel)),
                     reads=["jm", "Bs"], writes=[pSn], inc=(not use_sel))
                if use_sel:
                    ek, ekn = ekt[ecnt % 2], "ekt%d" % (ecnt % 2)
                    ecnt += 1
                    S.op("pool", lambda h, ek=ek, kt=kt: h.tensor_copy(out=ek[:], in_=eh[:, kt, :].unsqueeze(2).to_broadcast([128, 2, 64])),
                         reads=["eh"], writes=[ekn])
                    S.op("pe", lambda h, ek=ek, pS=pS: h.matmul(out=pS[:, :], lhsT=ek[:].rearrange("p a b -> p (a b)"),
                                                                rhs=mbT[:].rearrange("p h q -> p (h q)"), start=False, stop=True),
                         reads=[ekn, "mbT"], writes=[pSn])
                if sg == 0 and kt < 3:
                    S.op("act", lambda h, pS=pS, pT=pT, kt=kt: h.activation(out=pT[:, :], in_=pS[:, :], func=AF.Exp, bias=vbias[:, kt:kt + 1]),
                         reads=[pSn, "vbias"], writes=[pTn])
                else:
                    S.op("act", lambda h, pS=pS, pT=pT: h.activation(out=pT[:, :], in_=pS[:, :], func=AF.Exp), reads=[pSn], writes=[pTn])
                attend_tail(None, pT, pTn, vs_sb[:, kt, g, :], "vs_sb", kt == 0, kt == qt)
            finish_branch(g, 1, True)
        S.op("dve", lambda h: h.tensor_tensor(out=mix[:, 0:512], in0=sz[:, 0:512], in1=onsa[:].rearrange("p h d -> p (h d)"), op=ALU.mult),
             reads=["sq", "onsa"], writes=["hb"])
        S.op("dve", lambda h: h.tensor_tensor(out=mix[:, 512:1024], in0=sz[:, 512:1024], in1=of_sb[:], op=ALU.mult),
             reads=["sq", "of_sb"], writes=["hb"])
        for c in range(8):
            S.op("pe", lambda h, c=c: h.transpose(out=psT[:, c * 128:(c + 1) * 128], in_=mix[:, c * 128:(c + 1) * 128], identity=ident[:]),
                 reads=["hb", "ident"], writes=["psT"], inc=(c == 7))
        S.op("act", lambda h: h.copy(out=mixT[:].rearrange("p c t -> p (c t)"), in_=psT[:]), reads=["psT"], writes=["hT"])
        for bk in range(2):
            for c in range(8):
                S.op("pe", lambda h, bk=bk, c=c: h.matmul(out=psU[bk][:, :], lhsT=mixT[:, c, :], rhs=wo_bf[:, c, bk * 512:(bk + 1) * 512],
                                                         start=(c == 0), stop=(c == 7)),
                     reads=["hT", "wo_bf"], writes=["psU%d" % bk], inc=(c == 7))
            S.op("act", lambda h, bk=bk: h.copy(out=o2[:, bk * 512:(bk + 1) * 512], in_=psU[bk][:, :]), reads=["psU%d" % bk], writes=["o2"])
        S.op("dve", lambda h: h.tensor_tensor(out=sz[:], in0=o2[:], in1=o2[:], op=ALU.mult), reads=["o2"], writes=["sq"])
        S.op("dve", lambda h: h.reduce_sum(out=ssq3[:], in_=sz[:], axis=AX.X), reads=["sq"], writes=["ssq3"])
        S.op("dve", lambda h: h.tensor_scalar(out=rstd3[:], in0=ssq3[:], scalar1=1.0 / D, scalar2=RMS_EPS, op0=ALU.mult, op1=ALU.add),
             reads=["ssq3"], writes=["rstd3"])
        S.op("act", lambda h: h.activation(out=rstd3[:], in_=rstd3[:], func=AF.Sqrt), reads=["rstd3"], writes=["rstd3"])
        S.op("dve", lambda h: h.reciprocal(out=rstd3[:], in_=rstd3[:]), reads=["rstd3"], writes=["rstd3"])
        S.op("dve", lambda h: h.scalar_tensor_tensor(out=o2[:], in0=o2[:], scalar=rstd3[:, 0:1], in1=gpo[:], op0=ALU.mult, op1=ALU.mult),
             reads=["o2", "rstd3", "gbc"], writes=["o2"])
        S.op("dve", lambda h: h.tensor_tensor(out=x1[:], in0=o2[:], in1=xt3[:], op=ALU.add), reads=["o2", "xs0"], writes=["ckf"])
        S.op("pool", lambda h: h.tensor_copy(out=x1b[:], in_=x1[:]), reads=["ckf"], writes=["x1b"])
        for c in range(8):
            S.op("pe", lambda h, c=c: h.transpose(out=psT[:, c * 128:(c + 1) * 128], in_=x1b[:, c * 128:(c + 1) * 128], identity=ident[:]),
                 reads=["x1b", "ident"], writes=["psT"], inc=(c == 7))
        S.op("act", lambda h: h.copy(out=mixT[:].rearrange("p c t -> p (c t)"), in_=psT[:]), reads=["psT"], writes=["hT"])
        for bk in range(2):
            for c in range(8):
                S.op("pe", lambda h, bk=bk, c=c: h.matmul(out=psU[bk][:, :], lhsT=mixT[:, c, :], rhs=wg_bf[:, c, bk * 512:(bk + 1) * 512],
                                                         start=(c == 0), stop=(c == 7)),
                     reads=["hT", "wg_bf"], writes=["psU%d" % bk], inc=(c == 7))
            S.op("act", lambda h, bk=bk: h.activation(out=gate[:, bk * 512:(bk + 1) * 512], in_=psU[bk][:, :], func=AF.Sigmoid),
                 reads=["psU%d" % bk], writes=["xs1"])
        S.op("pool", lambda h: h.tensor_copy(out=pb3[:], in_=pt3[:]), reads=["pt3"], writes=["pb3"])
        for c in range(2):
            S.op("pe", lambda h, c=c: h.transpose(out=psT2[:, c * 128:(c + 1) * 128], in_=pb3[:, c * 128:(c + 1) * 128], identity=ident[:]),
                 reads=["pb3", "ident"], writes=["psT2"], inc=(c == 1))
        S.op("act", lambda h: h.copy(out=pT3[:].rearrange("p c t -> p (c t)"), in_=psT2[:, 0:256]), reads=["psT2"], writes=["pT3"])
        for bk in range(2):
            for c in range(2):
                S.op("pe", lambda h, bk=bk, c=c: h.matmul(out=psU[bk][:, :], lhsT=pT3[:, c, :], rhs=wp_bf[:, c, bk * 512:(bk + 1) * 512],
                                                         start=(c == 0), stop=(c == 1)),
                     reads=["pT3", "wp_bf"], writes=["psU%d" % bk], inc=(c == 1))
            S.op("dve", lambda h, bk=bk: h.tensor_tensor(out=yt[:, bk * 512:(bk + 1) * 512], in0=psU[bk][:, :],
                                                         in1=gate[:, bk * 512:(bk + 1) * 512], op=ALU.mult),
                 reads=["psU%d" % bk, "xs1"], writes=["yt"])
        S.op("dve", lambda h: h.tensor_tensor(out=yt[:], in0=yt[:], in1=x1[:], op=ALU.add), reads=["yt", "ckf"], writes=["yt"])
        S.dma("sp", o_y[i], yt[:], reads=["yt"], writes=["o_y"])

    S.wait_all("sp", ["o_kvc", "o_kvs", "o_kvw", "o_kvf", "o_logf", "o_wins", "o_y"])

    with nc.Block() as block:
        S.emit(block)
    stack.close()
    return nc


_NC_CACHE = {}


def kernel(x_prompt, x_sample, cache_cmp_kv, cache_slc_kv, cache_fox_kv, cache_fox_logf, state_win_kv,
           page_table, p_prompt, p_sample, g_pre, g_post, w_in, b_fgate, cmp_pos, w_cmp1, w_cmp2,
           w_out, w_pproj, w_pgate, t5_table):
    f32 = np.float32
    bf = ml_dtypes.bfloat16
    x_prompt = np.asarray(x_prompt, f32)
    x_sample = np.asarray(x_sample, f32)
    p_prompt = np.asarray(p_prompt, f32)
    p_sample = np.asarray(p_sample, f32)
    B, T, _ = x_prompt.shape
    if "nc" not in _NC_CACHE:
        _NC_CACHE["nc"] = build_nc()
    nc = _NC_CACHE["nc"]
    ident = np.eye(128, dtype=f32).astype(bf)
    ar = np.arange(128)
    tri = (ar[:, None] <= ar[None, :]).astype(f32)
    lastsel = np.zeros((128, 128), f32)
    lastsel[127, :] = 1.0
    diagmask = np.where(ar[:, None] <= ar[None, :], 0.0, NEG).astype(f32).astype(bf)
    page_table = np.asarray(page_table, np.int32)
    if NPOOL != 2560:
        cache_fox_kv = np.asarray(cache_fox_kv)[:NPOOL]
        cache_fox_logf = np.asarray(cache_fox_logf)[:NPOOL]
        cache_cmp_kv = np.asarray(cache_cmp_kv)[:NPOOL]
        cache_slc_kv = np.asarray(cache_slc_kv)[:NPOOL]
        page_table = page_table % NPOOL
    c_fox = np.ascontiguousarray(np.asarray(cache_fox_kv, f32)).reshape(-1, 1024)
    RW = 768
    rel = np.arange(RW) - 127
    nn = np.maximum(rel, 0)
    nf = np.maximum(nn, 16).astype(f32)
    large = 16 + (np.log(nf / f32(16)) / f32(np.log(64.0)) * f32(16)).astype(np.int32)
    bucket = np.where(nn < 16, nn, np.minimum(large, 31))
    inwin = (rel >= 0) & (rel < 512)
    ohw = np.zeros((33, RW), f32)
    ohw[bucket[inwin], np.nonzero(inwin)[0]] = 1.0
    ohw[32, ~inwin] = NEG
    tblx = np.concatenate([np.asarray(t5_table, f32), np.ones((1, 8), f32)], axis=0)
    jmat = np.eye(128, dtype=f32)[::-1].copy().astype(bf)
    c_cmp = np.ascontiguousarray(np.asarray(cache_cmp_kv, f32)).reshape(-1, 256)
    c_slc = np.ascontiguousarray(np.asarray(cache_slc_kv, f32)).reshape(-1, 256)

    def t5b(rel_):
        n_ = np.maximum(rel_, 0)
        nf_ = np.maximum(n_, 16).astype(f32)
        lg_ = 16 + (np.log(nf_ / f32(16)) / f32(np.log(64.0)) * f32(16)).astype(np.int32)
        return np.where(n_ < 16, n_, np.minimum(lg_, 31))

    def causal_onehot(rel_):
        oh = np.zeros((33, rel_.shape[0]), f32)
        ok = rel_ >= 0
        oh[t5b(rel_)[ok], np.nonzero(ok)[0]] = 1.0
        oh[32, ~ok] = NEG
        return oh

    RS, RC, OFFC = 1536, 16384, 7936
    ohs_s = causal_onehot(np.arange(RS) - 127)
    ohs_c = causal_onehot(np.arange(RC) - OFFC)
    mmat = np.zeros((128, 4, 128), f32)
    for n_ in range(511):
        for i_ in (n_, n_ + 1):
            mmat[n_ % 128, n_ // 128, i_ // 4] += 1.0
    ehalf = np.zeros((128, NT, 2), f32)
    for kt_ in range(64):
        ehalf[2 * kt_, kt_, 0] = 1.0
        ehalf[2 * kt_ + 1, kt_, 1] = 1.0
    blk_ = np.arange(128)[None, :]
    ii_ = np.arange(128)[:, None]
    c_lf = np.ascontiguousarray(np.asarray(cache_fox_logf, f32)).reshape(-1, 8)
    common = {
        "w_in": np.ascontiguousarray(np.asarray(w_in, f32)[0]),
        "g_pre": np.asarray(g_pre, f32).reshape(1, D),
        "g_post": np.asarray(g_post, f32).reshape(1, D),
        "b_fg": np.asarray(b_fgate, f32).reshape(1, 8),
        "ident": ident, "tri": tri, "lastsel": lastsel, "diagmask": diagmask,
        "c_fox": c_fox, "c_lf": c_lf, "ohw": ohw, "tblx": tblx, "jmat": jmat,
        "ohs_s": ohs_s, "ohs_c": ohs_c, "c_cmp": c_cmp, "c_slc": c_slc,
        "w_cmp1": np.ascontiguousarray(np.asarray(w_cmp1, f32)[0]), "w_cmp2": np.ascontiguousarray(np.asarray(w_cmp2, f32)[0]),
        "cmp_pos": np.ascontiguousarray(np.asarray(cmp_pos, f32)[0]),
        "mmat": mmat.astype(bf), "ehalf": ehalf.astype(bf),
        "w_out": np.ascontiguousarray(np.asarray(w_out, f32)[0]),
        "w_pgate": np.ascontiguousarray(np.asarray(w_pgate, f32)[0]),
        "w_pproj": np.ascontiguousarray(np.asarray(w_pproj, f32)[0]),
    }
    in_maps = []
    lay, tot = _pack_layout()
    base_pack = [None]
    for c in range(8):
        b, j = c // 4, c % 4
        sh = 3 - j
        xb = x_prompt[b].reshape(T // 128, 128, D)
        xv = np.zeros((64, 128, D), f32)
        xv[sh:] = xb[:64 - sh]
        valid = np.zeros((128, 64), f32)
        valid[:, sh:] = 1.0
        xsamp = np.zeros((NS, 128, D), f32)
        pown = np.zeros((NBLK, 128, 256), f32)
        pown[:NB_P] = p_prompt[0, b].reshape(T // 128, 128, 256)[j::4]
        for s in range(NS):
            xsamp[s, 0:4] = x_sample[NS * c + s]
            pown[NB_P + s, 0:4] = p_sample[0, NS * c + s]
        keep = np.zeros((NBLK, 128, 128), f32)
        addc = np.zeros((NBLK, 128, 128), f32)
        for i_own in range(NBLK):
            if i_own < NB_P:
                cur = (128 * (4 * i_own + 3) + ii_) // 64
                blk0, npad = 2 * sh, 2 * sh
            else:
                cur = np.full((128, 1), 128)
                blk0, npad = 0, 0
            a_ = np.zeros((128, 128), f32)
            a_ = np.where(blk_ == cur - 1, 1e9, a_)
            a_ = np.where(blk_ == cur, 2e9, a_)
            a_ = np.where(blk_ == blk0, 3e9, a_)
            a_ = np.where((blk_ > cur) | (blk_ < npad), -1e30, a_)
            addc[i_own] = a_
            keep[i_own] = (a_ == 0)
        nn_ = 128 * np.arange(4)[None, :] + np.arange(128)[:, None]
        vbc = np.zeros((128, 2, 4), f32)
        vbc[:, 0, :] = np.where((nn_ < 8 * sh) | (nn_ >= 511), NEG, 0.0)
        vbc[:, 1, :] = np.where(nn_ >= 511, NEG, 0.0)
        m = dict(common)
        m.update({"keep": keep, "addc": addc.astype(f32), "vbc": vbc})
        m.update({
            "xv": xv, "xsamp": xsamp, "valid": valid, "pown": pown,
            "swin": np.ascontiguousarray(np.asarray(state_win_kv, f32)[NS * c:NS * c + NS, 0].reshape(NS, 512, 256)),
            "ptab": np.ascontiguousarray(page_table[NS * c:NS * c + NS]),
        })
        sep = ("xv", "c_fox", "c_lf", "c_cmp", "c_slc", "ptab")
        if base_pack[0] is None:
            bpf = np.zeros(tot["f"], f32)
            bpb = np.zeros(tot["b"], bf)
            for name_, (k_, off_, shp_) in lay.items():
                if name_ in common:
                    arr_ = np.asarray(common[name_])
                    assert arr_.shape == shp_, (name_, arr_.shape, shp_)
                    (bpf if k_ == "f" else bpb)[off_:off_ + arr_.size] = arr_.reshape(-1)
            base_pack[0] = (bpf, bpb)
        pf, pb = base_pack[0][0].copy(), base_pack[0][1].copy()
        for name_, (k_, off_, shp_) in lay.items():
            if name_ not in common:
                arr_ = np.asarray(m[name_])
                assert arr_.shape == shp_, (name_, arr_.shape, shp_)
                (pf if k_ == "f" else pb)[off_:off_ + arr_.size] = arr_.reshape(-1).astype(f32 if k_ == "f" else bf)
        mm = {k_: m[k_] for k_ in sep}
        mm["packf"], mm["packb"] = pf, pb
        in_maps.append(mm)
    res = run_bass_kernel_spmd(nc, in_maps, core_ids=list(range(8)))
    R = res.results

    y_p = np.zeros((B, T, D), f32)
    y_s = np.zeros((32, 4, D), f32)
    cmp_p = np.zeros((B, 1, T, 2, 2, 64), f32)
    slc_p = np.zeros((B, 1, T, 2, 2, 64), f32)
    fox_p = np.zeros((B, 1, T, 2, 8, 64), f32)
    lf_p = np.zeros((B, 1, T, 8), f32)
    win_p = np.zeros((B, 1, 512, 2, 2, 64), f32)
    cmp_s = np.zeros((32, 1, 4, 2, 2, 64), f32)
    slc_s = np.zeros((32, 1, 4, 2, 2, 64), f32)
    fox_s = np.zeros((32, 1, 4, 2, 8, 64), f32)
    lf_s = np.zeros((32, 1, 4, 8), f32)
    win_s = np.zeros((32, 1, 512, 2, 2, 64), f32)
    for c in range(8):
        b, j = c // 4, c % 4
        r = R[c]
        kvc, kvs, kvw = np.asarray(r["o_kvc"]), np.asarray(r["o_kvs"]), np.asarray(r["o_kvw"])
        kvf, lf, wins, yy = np.asarray(r["o_kvf"]), np.asarray(r["o_logf"]), np.asarray(r["o_wins"]), np.asarray(r["o_y"])
        y_p[b].reshape(T // 128, 128, D)[j::4] = yy[:NB_P]
        cmp_p[b, 0].reshape(T // 128, 128, 256)[j::4] = kvc[:NB_P]
        slc_p[b, 0].reshape(T // 128, 128, 256)[j::4] = kvs[:NB_P]
        fox_p[b, 0].reshape(T // 128, 128, 1024)[j::4] = kvf[:NB_P]
        lf_p[b, 0].reshape(T // 128, 128, 8)[j::4] = lf[:NB_P]
        win_p[b, 0].reshape(4, 128, 256)[j] = kvw[NB_P - 1]
        for s in range(NS):
            q = NS * c + s
            y_s[q] = yy[NB_P + s, 0:4]
            cmp_s[q, 0] = kvc[NB_P + s, 0:4].reshape(4, 2, 2, 64)
            slc_s[q, 0] = kvs[NB_P + s, 0:4].reshape(4, 2, 2, 64)
            fox_s[q, 0] = kvf[NB_P + s, 0:4].reshape(4, 2, 8, 64)
            lf_s[q, 0] = lf[NB_P + s, 0:4]
            win_s[q, 0] = wins[s].reshape(512, 2, 2, 64)
    return (y_p, y_s, cmp_p, slc_p, fox_p, lf_p, win_p, cmp_s, slc_s, fox_s, lf_s, win_s)
```
